# Optimizing a Trainium2 kernel written in Bass

```python
import math
import jax, jax.numpy as jnp
from jax import lax
import numpy as np

D_MODEL = 2048
BATCH = 4
SEQ = 2048
DEPTH = 4
DEC_BATCH = 128
DEC_SEQ = 1
PAST_LEN = 16384
PAGE_SIZE = 128

N_MIXERS = 2
N_SG_LAYERS = (DEPTH + 1) // 2
N_SSM_LAYERS = DEPTH // 2
MIX_WIDTH = D_MODEL
XA_HEADS = 4
XA_HEAD_DIM = D_MODEL // 16
XA_WIDTH = XA_HEADS * XA_HEAD_DIM
N_MEM = 256
TOK_WIDTH = MIX_WIDTH - XA_WIDTH
CHUNK = 128
SG_WIDTH = TOK_WIDTH
SG_GROUP_DIM = 128
SG_GROUPS = SG_WIDTH // SG_GROUP_DIM
SSM_WIDTH = TOK_WIDTH
SSM_GROUP_DIM = 16
SSM_GROUPS = SSM_WIDTH // SSM_GROUP_DIM
SSM_STATE = 64
DT_MIN = 1e-3
DT_MAX = 1e-1
D_FF = ((8 * D_MODEL // 3 + 127) // 128) * 128
CONV_W = 3
EPS = 1e-6

kernel_name = "hybrid_sgmlp_s5_memxattn_convffn_step"


def rmsnorm(x, g):
    x32 = x.astype(jnp.float32)
    y = x32 * lax.rsqrt(jnp.mean(x32 * x32, axis=-1, keepdims=True) + EPS)
    return y.astype(x.dtype) * g


def mem_kv(mem, g, w):
    k, v = jnp.split(rmsnorm(mem, g) @ w, 2, axis=-1)
    b, m = mem.shape[0], mem.shape[1]
    return (k.reshape(b, m, XA_HEADS, XA_HEAD_DIM), v.reshape(b, m, XA_HEADS, XA_HEAD_DIM))


def cross_attend(q, k, v):
    b, l = q.shape[0], q.shape[1]
    qh = q.reshape(b, l, XA_HEADS, XA_HEAD_DIM)
    s = jnp.einsum('blhd,bmhd->bhlm', qh, k).astype(jnp.float32) * (XA_HEAD_DIM ** -0.5)
    p = jax.nn.softmax(s, axis=-1).astype(v.dtype)
    o = jnp.einsum('bhlm,bmhd->blhd', p, v)
    return o.reshape(b, l, XA_WIDTH)


def spatial_gate(v, w_s, b_s):
    b, l, w = v.shape
    pad = (-l) % CHUNK
    vp = jnp.pad(v, ((0, 0), (0, pad), (0, 0)))
    n = vp.shape[1] // CHUNK
    vc = vp.reshape(b, n, CHUNK, SG_GROUPS, SG_GROUP_DIM)
    mask = jnp.tril(jnp.ones((CHUNK, CHUNK), dtype=bool))
    wm = jnp.where(mask, w_s, 0)
    s = jnp.einsum('gts,bnsgd->bntgd', wm, vc) + b_s.T[:, :, None]
    return s.reshape(b, n * CHUNK, w)[:, :l]


def sg_mixer(h, w_in, w_out, g_v, w_s, b_s, mk, mv):
    z = h @ w_in
    uv, q = z[..., :2 * SG_WIDTH], z[..., 2 * SG_WIDTH:]
    u, v = jnp.split(jax.nn.gelu(uv), 2, axis=-1)
    v = rmsnorm(v, g_v)
    tok = u * spatial_gate(v, w_s, b_s)
    out = jnp.concatenate([tok, cross_attend(q, mk, mv)], axis=-1) @ w_out
    return out, v


def s5_scan(u, s_re, s_im, lam_re, lam_im, log_dt, b_re, b_im, c_re, c_im, d):
    f = jnp.float32
    dtype = u.dtype
    bsz, l = u.shape[0], u.shape[1]
    lam_re, lam_im = lam_re.astype(f), lam_im.astype(f)
    dt = jnp.exp(log_dt.astype(f))[:, None]
    ar, ai = lam_re * dt, lam_im * dt
    mag = jnp.exp(ar)
    lb_re, lb_im = mag * jnp.cos(ai), mag * jnp.sin(ai)
    nr, ni = lb_re - 1.0, lb_im
    den = lam_re * lam_re + lam_im * lam_im
    k_re = (nr * lam_re + ni * lam_im) / den
    k_im = (ni * lam_re - nr * lam_im) / den
    ug = u.astype(f).reshape(bsz, l, SSM_GROUPS, SSM_GROUP_DIM)
    bu_re = jnp.einsum('blhc,hpc->blhp', ug, b_re.astype(f))
    bu_im = jnp.einsum('blhc,hpc->blhp', ug, b_im.astype(f))
    x_re = k_re * bu_re - k_im * bu_im
    x_im = k_re * bu_im + k_im * bu_re
    a_re = jnp.broadcast_to(lb_re, x_re.shape)
    a_im = jnp.broadcast_to(lb_im, x_im.shape)

    def combine(e1, e2):
        a1r, a1i, b1r, b1i = e1
        a2r, a2i, b2r, b2i = e2
        return (a2r * a1r - a2i * a1i, a2r * a1i + a2i * a1r,
                a2r * b1r - a2i * b1i + b2r, a2r * b1i + a2i * b1r + b2i)

    _, _, h_re, h_im = lax.associative_scan(combine, (a_re, a_im, x_re, x_im), axis=1)
    steps = jnp.arange(1, l + 1, dtype=f)[:, None, None]
    pmag = jnp.exp(ar * steps)
    p_re, p_im = pmag * jnp.cos(ai * steps), pmag * jnp.sin(ai * steps)
    s0r, s0i = s_re.astype(f)[:, None], s_im.astype(f)[:, None]
    h_re = h_re + p_re * s0r - p_im * s0i
    h_im = h_im + p_re * s0i + p_im * s0r
    y = (jnp.einsum('blhp,hcp->blhc', h_re, c_re.astype(f))
         - jnp.einsum('blhp,hcp->blhc', h_im, c_im.astype(f)))
    y = y.reshape(bsz, l, SSM_WIDTH) + d.astype(f) * u.astype(f)
    return y.astype(dtype), h_re[:, -1].astype(s_re.dtype), h_im[:, -1].astype(s_im.dtype)


def ssm_mixer(h, w_in, w_out, lam_re, lam_im, log_dt, b_re, b_im, c_re, c_im, d,
              w_glu, b_glu, mk, mv, s_re, s_im):
    z = h @ w_in
    u, q = z[..., :SSM_WIDTH], z[..., SSM_WIDTH:]
    y, n_re, n_im = s5_scan(u, s_re, s_im, lam_re, lam_im, log_dt, b_re, b_im, c_re, c_im, d)
    y = jax.nn.gelu(y)
    y = y * jax.nn.sigmoid(y @ w_glu + b_glu)
    out = jnp.concatenate([y, cross_attend(q, mk, mv)], axis=-1) @ w_out
    return out, n_re, n_im


def conv_ffn(h, w_up, conv_w, conv_b, w_down, prev):
    a, g = jnp.split(h @ w_up, 2, axis=-1)
    l = a.shape[1]
    full = jnp.concatenate([prev.astype(a.dtype), a], axis=1)
    c = conv_b
    for j in range(CONV_W):
        c = c + conv_w[j] * full[:, j:j + l]
    y = jax.nn.silu(c) * g
    return y @ w_down, full[:, l:]


def trunk(x, mem_k, mem_v, ssm_re, ssm_im, conv_prev, p):
    sg_v, s_re, s_im, conv_new = [], [], [], []
    for i in range(DEPTH):
        j = i // N_MIXERS
        h = rmsnorm(x, p['g_mix'][i])
        if i % N_MIXERS == 0:
            out, v = sg_mixer(h, p['sg_w_in'][j], p['sg_w_out'][j], p['sg_g_v'][j],
                              p['sg_w_s'][j], p['sg_b_s'][j], mem_k[i], mem_v[i])
            sg_v.append(v)
        else:
            out, r, im = ssm_mixer(h, p['ssm_w_in'][j], p['ssm_w_out'][j], p['ssm_lam_re'][j],
                                   p['ssm_lam_im'][j], p['ssm_log_dt'][j], p['ssm_b_re'][j],
                                   p['ssm_b_im'][j], p['ssm_c_re'][j], p['ssm_c_im'][j],
                                   p['ssm_d'][j], p['ssm_w_glu'][j], p['ssm_b_glu'][j],
                                   mem_k[i], mem_v[i], ssm_re[j], ssm_im[j])
            s_re.append(r)
            s_im.append(im)
        x = x + out
        h = rmsnorm(x, p['g_ffn'][i])
        out, c = conv_ffn(h, p['ffn_w_up'][i], p['ffn_conv_w'][i], p['ffn_conv_b'][i],
                          p['ffn_w_down'][i], conv_prev[i])
        conv_new.append(c)
        x = x + out
    return (rmsnorm(x, p['g_final']), jnp.stack(sg_v), jnp.stack(s_re), jnp.stack(s_im),
            jnp.stack(conv_new))


def setup_inputs(seed: int = 0) -> dict:
    key = jax.random.key(seed)
    ks = iter(jax.random.split(key, 48))
    f = jnp.float32

    def nrm(shape, scale):
        return jax.random.normal(next(ks), shape, f) * scale

    def gain(shape):
        return 1.0 + nrm(shape, 0.01)

    lam_im = (jnp.broadcast_to(jnp.pi * jnp.arange(SSM_STATE, dtype=f), (N_SSM_LAYERS, SSM_GROUPS, SSM_STATE))
              + nrm((N_SSM_LAYERS, SSM_GROUPS, SSM_STATE), 0.01))
    return {
        'x_prompt': nrm((BATCH, SEQ, D_MODEL), 1.0),
        'x_sample': nrm((DEC_BATCH, DEC_SEQ, D_MODEL), 1.0),
        'mem_prompt': nrm((BATCH, N_MEM, D_MODEL), 1.0),
        'cache_mem_k': nrm((DEPTH, DEC_BATCH, N_MEM, XA_HEADS, XA_HEAD_DIM), 1.0),
        'cache_mem_v': nrm((DEPTH, DEC_BATCH, N_MEM, XA_HEADS, XA_HEAD_DIM), 1.0),
        'state_ssm_re': nrm((N_SSM_LAYERS, DEC_BATCH, SSM_GROUPS, SSM_STATE), 0.1),
        'state_ssm_im': nrm((N_SSM_LAYERS, DEC_BATCH, SSM_GROUPS, SSM_STATE), 0.1),
        'state_conv': nrm((DEPTH, DEC_BATCH, CONV_W - 1, D_FF), 1.0),
        'g_mix': gain((DEPTH, D_MODEL)),
        'g_ffn': gain((DEPTH, D_MODEL)),
        'g_mem': gain((DEPTH, D_MODEL)),
        'g_final': gain((D_MODEL,)),
        'w_mem_kv': nrm((DEPTH, D_MODEL, 2 * XA_WIDTH), D_MODEL ** -0.5),
        'sg_w_in': nrm((N_SG_LAYERS, D_MODEL, 2 * SG_WIDTH + XA_WIDTH), D_MODEL ** -0.5),
        'sg_w_out': nrm((N_SG_LAYERS, SG_WIDTH + XA_WIDTH, D_MODEL), (SG_WIDTH + XA_WIDTH) ** -0.5),
        'sg_g_v': gain((N_SG_LAYERS, SG_WIDTH)),
        'sg_w_s': nrm((N_SG_LAYERS, SG_GROUPS, CHUNK, CHUNK), CHUNK ** -0.5),
        'sg_b_s': gain((N_SG_LAYERS, SG_GROUPS, CHUNK)),
        'ssm_w_in': nrm((N_SSM_LAYERS, D_MODEL, SSM_WIDTH + XA_WIDTH), D_MODEL ** -0.5),
        'ssm_w_out': nrm((N_SSM_LAYERS, SSM_WIDTH + XA_WIDTH, D_MODEL), (SSM_WIDTH + XA_WIDTH) ** -0.5),
        'ssm_lam_re': -0.5 + nrm((N_SSM_LAYERS, SSM_GROUPS, SSM_STATE), 0.01),
        'ssm_lam_im': lam_im,
        'ssm_log_dt': jax.random.uniform(next(ks), (N_SSM_LAYERS, SSM_GROUPS), f,
                                         math.log(DT_MIN), math.log(DT_MAX)),
        'ssm_b_re': nrm((N_SSM_LAYERS, SSM_GROUPS, SSM_STATE, SSM_GROUP_DIM), (2 * SSM_GROUP_DIM) ** -0.5),
        'ssm_b_im': nrm((N_SSM_LAYERS, SSM_GROUPS, SSM_STATE, SSM_GROUP_DIM), (2 * SSM_GROUP_DIM) ** -0.5),
        'ssm_c_re': nrm((N_SSM_LAYERS, SSM_GROUPS, SSM_GROUP_DIM, SSM_STATE), SSM_STATE ** -0.5),
        'ssm_c_im': nrm((N_SSM_LAYERS, SSM_GROUPS, SSM_GROUP_DIM, SSM_STATE), SSM_STATE ** -0.5),
        'ssm_d': nrm((N_SSM_LAYERS, SSM_WIDTH), 1.0),
        'ssm_w_glu': nrm((N_SSM_LAYERS, SSM_WIDTH, SSM_WIDTH), SSM_WIDTH ** -0.5),
        'ssm_b_glu': nrm((N_SSM_LAYERS, SSM_WIDTH), 0.01),
        'ffn_w_up': nrm((DEPTH, D_MODEL, 2 * D_FF), D_MODEL ** -0.5),
        'ffn_conv_w': nrm((DEPTH, CONV_W, D_FF), CONV_W ** -0.5),
        'ffn_conv_b': nrm((DEPTH, D_FF), 0.01),
        'ffn_w_down': nrm((DEPTH, D_FF, D_MODEL), D_FF ** -0.5),
    }


def reference(x_prompt, x_sample, mem_prompt, cache_mem_k, cache_mem_v, state_ssm_re, state_ssm_im,
              state_conv, g_mix, g_ffn, g_mem, g_final, w_mem_kv, sg_w_in, sg_w_out, sg_g_v, sg_w_s,
              sg_b_s, ssm_w_in, ssm_w_out, ssm_lam_re, ssm_lam_im, ssm_log_dt, ssm_b_re, ssm_b_im,
              ssm_c_re, ssm_c_im, ssm_d, ssm_w_glu, ssm_b_glu, ffn_w_up, ffn_conv_w, ffn_conv_b,
              ffn_w_down):
    p = {'g_mix': g_mix, 'g_ffn': g_ffn, 'g_final': g_final,
         'sg_w_in': sg_w_in, 'sg_w_out': sg_w_out, 'sg_g_v': sg_g_v, 'sg_w_s': sg_w_s, 'sg_b_s': sg_b_s,
         'ssm_w_in': ssm_w_in, 'ssm_w_out': ssm_w_out, 'ssm_lam_re': ssm_lam_re, 'ssm_lam_im': ssm_lam_im,
         'ssm_log_dt': ssm_log_dt, 'ssm_b_re': ssm_b_re, 'ssm_b_im': ssm_b_im, 'ssm_c_re': ssm_c_re,
         'ssm_c_im': ssm_c_im, 'ssm_d': ssm_d, 'ssm_w_glu': ssm_w_glu, 'ssm_b_glu': ssm_b_glu,
         'ffn_w_up': ffn_w_up, 'ffn_conv_w': ffn_conv_w, 'ffn_conv_b': ffn_conv_b, 'ffn_w_down': ffn_w_down}
    bsz = x_prompt.shape[0]
    dt = x_prompt.dtype
    kv = [mem_kv(mem_prompt, g_mem[i], w_mem_kv[i]) for i in range(DEPTH)]
    mem_k_prompt = jnp.stack([k for k, _ in kv])
    mem_v_prompt = jnp.stack([v for _, v in kv])
    zero_re = jnp.zeros((N_SSM_LAYERS, bsz, SSM_GROUPS, SSM_STATE), dt)
    zero_im = jnp.zeros((N_SSM_LAYERS, bsz, SSM_GROUPS, SSM_STATE), dt)
    zero_conv = jnp.zeros((DEPTH, bsz, CONV_W - 1, D_FF), dt)
    y_prompt, _, ssm_re_prompt, ssm_im_prompt, conv_prompt = trunk(
        x_prompt, mem_k_prompt, mem_v_prompt, zero_re, zero_im, zero_conv, p)
    y_sample, sg_v_sample, ssm_re_sample, ssm_im_sample, conv_sample = trunk(
        x_sample, cache_mem_k, cache_mem_v, state_ssm_re, state_ssm_im, state_conv, p)
    return (y_prompt, y_sample, mem_k_prompt, mem_v_prompt, ssm_re_prompt, ssm_im_prompt, conv_prompt,
            ssm_re_sample, ssm_im_sample, conv_sample, sg_v_sample)
```

```python
import contextlib
import os
import types
import numpy as np
import concourse.bass as bass
import concourse.mybir as mybir
from concourse.bass_utils import run_bass_kernel_spmd

F32 = mybir.dt.float32
BF16 = mybir.dt.bfloat16
I32 = mybir.dt.int32
AF = mybir.ActivationFunctionType
ALU = mybir.AluOpType
AX = mybir.AxisListType

D = 2048; DEPTH = 4; SEQ = 2048; NB = 4; NS = 128; NMEM = 256
TW = 1536; DFF = 5504; FC = 43; NG = 96
TP = 512; SS = 16; TT = TP + SS
NPASS = SEQ // TP
EPS = 1e-6
TWO_PI = float(2 * np.pi)


def _freeze(fn):
    if fn is None or fn.__closure__ is None:
        return fn
    cells = []
    for c in fn.__closure__:
        try:
            cells.append(types.CellType(c.cell_contents))
        except ValueError:
            cells.append(c)
    return types.FunctionType(fn.__code__, fn.__globals__, fn.__name__, fn.__defaults__, tuple(cells))


class Sched:
    ENG = ('pe', 'act', 'dve', 'pool', 'sp')

    def __init__(self, nc, stack, n_dma_sems=20):
        self.nc = nc
        self.prog = {e: [] for e in self.ENG}
        self.sem = {e: stack.enter_context(nc.semaphore('S_' + e)) for e in ('pe', 'act', 'dve', 'pool')}
        self.cnt = {e: 0 for e in self.sem}
        self.dsem = {q: [stack.enter_context(nc.semaphore('D_%s%d' % (q, i))) for i in range(n_dma_sems)]
                     for q in ('sp', 'pool')}
        self.dcnt = {q: [0] * n_dma_sems for q in ('sp', 'pool')}
        self.dnext = {q: 0 for q in ('sp', 'pool')}
        self.waited = {e: {} for e in self.ENG}
        self.res = {}
        self.nins = 0
        self.stopped = False

    def _need(self, eng, tok, waits):
        if tok is None:
            return
        sem, val = tok
        w = self.waited[eng]
        if w.get(sem.name, 0) >= val:
            return
        w[sem.name] = val
        waits.append((sem, val))

    def _deps(self, eng, reads, writes, waits):
        for k in reads:
            r = self.res.get(k)
            if r:
                self._need(eng, r['w'], waits)
                if isinstance(k, tuple) and k[0] == 'ps':
                    for sname, t in r['r'].items():
                        if sname != self.sem.get(eng, None) and (eng not in self.sem or sname != self.sem[eng].name):
                            self._need(eng, t, waits)
        for k in writes:
            r = self.res.get(k)
            if r:
                self._need(eng, r['w'], waits)
                for t in r['r'].values():
                    self._need(eng, t, waits)

    def _commit(self, tok, reads, writes):
        for k in reads:
            r = self.res.setdefault(k, {'w': None, 'r': {}})
            r['r'][tok[0].name] = tok
        for k in writes:
            self.res[k] = {'w': tok, 'r': {}}

    def op(self, eng, fns, reads=(), writes=()):
        if self.stopped:
            return None
        if not isinstance(fns, (list, tuple)):
            fns = [fns]
        fns = [_freeze(f) for f in fns]
        waits = []
        self._deps(eng, reads, writes, waits)
        self.cnt[eng] += 1
        tok = (self.sem[eng], self.cnt[eng])
        n = len(fns)
        for i, fn in enumerate(fns):
            self.prog[eng].append((fn, waits if i == 0 else [], (self.sem[eng], 1) if i == n - 1 else None))
        self.nins += n
        self._commit(tok, reads, writes)
        return tok

    def dma(self, q, fn, reads=(), writes=()):
        if self.stopped:
            return None
        waits = []
        i = self.dnext[q]
        self.dnext[q] = (i + 1) % len(self.dsem[q])
        sem = self.dsem[q][i]
        if self.dcnt[q][i] > 0:
            self._need(q, (sem, self.dcnt[q][i]), waits)
        self._deps(q, reads, writes, waits)
        self.dcnt[q][i] += 16
        tok = (sem, self.dcnt[q][i])
        self.prog[q].append((_freeze(fn), waits, (sem, 16)))
        self.nins += 1
        self._commit(tok, reads, writes)
        return tok

    def all_tokens(self):
        toks = [(self.sem[e], self.cnt[e]) for e in self.sem if self.cnt[e] > 0]
        for q in self.dsem:
            for i, s in enumerate(self.dsem[q]):
                if self.dcnt[q][i] > 0:
                    toks.append((s, self.dcnt[q][i]))
        return toks

    def barrier(self, keep=lambda k: False):
        if self.stopped:
            return
        toks = [(self.sem[e], self.cnt[e]) for e in ('pe', 'act', 'dve') if self.cnt[e] > 0]
        for q in ('sp', 'pool'):
            for i, s in enumerate(self.dsem[q]):
                if self.dcnt[q][i] > 0:
                    toks.append((s, self.dcnt[q][i]))
        for e in ('pe', 'act', 'dve', 'sp', 'pool'):
            waits = []
            for t in toks:
                self._need(e, t, waits)
            if waits:
                self.prog[e].append((None, waits, None))
        self.res = {k: v for k, v in self.res.items() if keep(k)}

    def emit(self, block):
        final = self.all_tokens()

        def run(e, engine):
            for fn, waits, inc in self.prog[e]:
                for sem, val in waits:
                    engine.wait_ge(sem, val)
                if fn is None:
                    continue
                ins = fn(engine)
                if inc is not None:
                    ins.then_inc(inc[0], inc[1])
            if e == 'sp':
                for sem, val in final:
                    engine.wait_ge(sem, val)

        @block.tensor
        def _(eng):
            run('pe', eng)

        @block.scalar
        def _(eng):
            run('act', eng)

        @block.vector
        def _(eng):
            run('dve', eng)

        @block.gpsimd
        def _(eng):
            run('pool', eng)

        @block.sync
        def _(eng):
            run('sp', eng)


C_ID = 0
C_JT = 128
C_TRI = 256
C_IOTA = 384
C_SGNC = 896
C_SGNB = 897
C_GMASK = 898
NCST = 906


def make_consts():
    c = np.zeros((128, NCST), np.float32)
    c[:, C_ID:C_ID + 128] = np.eye(128)
    J = np.zeros((128, 128), np.float32)
    for m in range(64):
        J[m, m + 64] = -1.0
        J[m + 64, m] = 1.0
    c[:, C_JT:C_JT + 128] = J.T
    s = np.arange(128)[:, None]; t = np.arange(128)[None, :]
    c[:, C_TRI:C_TRI + 128] = (t >= s).astype(np.float32)
    c[:, C_IOTA:C_IOTA + 512] = np.arange(512, dtype=np.float32)[None, :]
    c[:64, C_SGNC] = 1.0; c[64:, C_SGNC] = -1.0
    c[:64, C_SGNB] = -1.0; c[64:, C_SGNB] = 1.0
    for i in range(8):
        c[i * 16:(i + 1) * 16, C_GMASK + i] = 1.0
    return c


class _Stop(Exception):
    pass


def build_program():
    nc = bass.Bass("TRN2", target_bir_lowering=False)
    SMALLW = bool(os.environ.get('MK_SMALLW'))
    STOP = os.environ.get('MK_STOP', '')
    NPASS_ = int(os.environ.get('MK_NPASS', NPASS))

    def chk(name):
        if STOP == name and not S.stopped:
            S.dma('sp', lambda e: e.dma_start(out=o_dbgx, in_=x_[0][:].rearrange("p a b -> p (a b)")), reads=['x'])
            S.dma('pool', lambda e: e.dma_start(out=o_dbgh, in_=h_[0][:].rearrange("p a b -> p (a b)")), reads=['h'])
            if dbg_u[0] is not None:
                S.dma('sp', lambda e: e.dma_start(out=o_dbgu, in_=dbg_u[0].rearrange("p a b -> p (a b)")), reads=['u'] + [('u', m) for m in range(12)])
            S.stopped = True
    x_ = [None]; h_ = [None]

    class WSel:
        def __init__(self, ap):
            self.ap = ap

        def __getitem__(self, idx):
            if not isinstance(idx, tuple):
                idx = (idx,)
            if SMALLW:
                idx = tuple(0 for _ in idx)
            return self.ap[idx]

    def din(name, shape):
        return nc.dram_tensor(name, list(shape), F32, kind="ExternalInput").ap()

    def dout(name, shape):
        return nc.dram_tensor(name, list(shape), F32, kind="ExternalOutput").ap()

    xp = din('xp', [SEQ, D]); xs = din('xs', [SS, D]); mem = din('mem', [NMEM, D])
    ck = din('ck', [DEPTH, SS, NMEM, 512]); cv = din('cv', [DEPTH, SS, NMEM, 512])
    ss_cat = din('ss_cat', [2, SS, NG * 128]); ss_sw = din('ss_sw', [2, SS, NG * 128])
    sconv = din('sconv', [DEPTH, SS, 2, DFF])
    gvec = din('gvec', [128, 13 * 16])
    gv_bc = din('gv_bc', [2, 128, TW]); wsT = din('wsT', [2, 128, 12 * 128]); bs_bc = din('bs_bc', [2, 128, 12 * 128])
    w00 = din('w00', [2, 16, 12])
    lam = din('lam', [2, 128, 3 * NG])
    Bcat = din('Bcat', [2, 128, NG * 16]); Bsw = din('Bsw', [2, 128, NG * 16])
    Ccat = din('Ccat', [2, 128, NG * 16]); Csw = din('Csw', [2, 128, NG * 16])
    dvec = din('dvec', [2, 128, 12]); bglu = din('bglu', [2, 128, 12])
    convw = din('convw', [DEPTH, 128, FC * 4])
    cst = din('cst', [128, NCST])
    w_kv = WSel(din('w_kv', ([1, 1] + [DEPTH, 2, 128, 16 * 512][2:]) if SMALLW else [DEPTH, 2, 128, 16 * 512]))
    w_sgin = WSel(din('w_sgin', ([1, 1] + [2, 7, 128, 16 * 512][2:]) if SMALLW else [2, 7, 128, 16 * 512])); w_sgout = WSel(din('w_sgout', ([1, 1] + [2, 4, 128, 16 * 512][2:]) if SMALLW else [2, 4, 128, 16 * 512]))
    w_ssin = WSel(din('w_ssin', ([1, 1] + [2, 4, 128, 16 * 512][2:]) if SMALLW else [2, 4, 128, 16 * 512])); w_ssout = WSel(din('w_ssout', ([1, 1] + [2, 4, 128, 16 * 512][2:]) if SMALLW else [2, 4, 128, 16 * 512]))
    w_glu = WSel(din('w_glu', ([1, 1] + [2, 3, 128, 12 * 512][2:]) if SMALLW else [2, 3, 128, 12 * 512]))
    w_up = WSel(din('w_up', ([1, 1] + [DEPTH, 22, 128, 16 * 512][2:]) if SMALLW else [DEPTH, 22, 128, 16 * 512])); w_dn = WSel(din('w_dn', ([1, 1] + [DEPTH, 16, 128, FC * 128][2:]) if SMALLW else [DEPTH, 16, 128, FC * 128]))

    o_y = dout('o_y', [SEQ, D]); o_ys = dout('o_ys', [SS, D])
    o_mk = dout('o_mk', [DEPTH, NMEM, 512]); o_mv = dout('o_mv', [DEPTH, NMEM, 512])
    o_ssm_p = dout('o_ssm_p', [2, 128, NG]); o_convx = dout('o_convx', [DEPTH, 18, DFF])
    o_conv_s0 = dout('o_conv_s0', [DEPTH, SS, DFF]); o_ssm_s = dout('o_ssm_s', [2, SS, NG * 128])
    o_sgv = dout('o_sgv', [2, SS, TW])
    DBG = bool(os.environ.get('MK_STOP'))
    if DBG:
        o_dbgx = dout('o_dbgx', [128, 16 * TT]); o_dbgh = dout('o_dbgh', [128, 16 * TT]); o_dbgu = dout('o_dbgu', [128, 12 * TT])
    dbg_u = [None]

    with contextlib.ExitStack() as st:
        def sb(name, shape, dt=F32):
            return st.enter_context(nc.sbuf_tensor(name, list(shape), dt))

        x = sb('x', [128, 16, TT])
        h = sb('h', [128, 16, TT], BF16)
        x_[0] = x; h_[0] = h
        cs = sb('cs', [128, NCST])
        identb = sb('identb', [128, 128], BF16); onesb = sb('onesb', [128, 128], BF16)
        gv = sb('gv', [128, 13 * 16])
        epsb = sb('epsb', [128, 1]); hpib = sb('hpib', [128, 1])
        KT = sb('KT', [128, DEPTH, 4, NMEM], BF16); Vt = sb('Vt', [128, DEPTH, 2, 512], BF16)
        wr = [sb('wr%d' % i, [128, 16 * 512], BF16) for i in range(2)]
        halo = sb('halo', [128, DEPTH, FC, 2])
        cw = sb('cw', [128, DEPTH, FC * 4])
        th = sb('th', [128, 2, NG]); rho = sb('rho', [128, 2, NG])
        KR = sb('KR', [128, 2, NG]); KI = sb('KI', [128, 2, NG]); KRs = sb('KRs', [128, 2, NG]); KIs = sb('KIs', [128, 2, NG])
        c512 = sb('c512', [128, 2, NG]); s512 = sb('s512', [128, 2, NG]); c511 = sb('c511', [128, 2, NG]); s511 = sb('s511', [128, 2, NG])
        A1 = sb('A1', [128, 2, NG]); A2s = sb('A2s', [128, 2, NG])
        winit = sb('winit', [128, 2, NG])
        dv = sb('dv', [128, 2, 12]); bg = sb('bg', [128, 2, 12])
        rsd = sb('rsd', [128, TT]); rstd = sb('rstd', [128, TT])
        ARENA = 86 * 1024
        arena = sb('arena', [128, ARENA // 2], BF16)
        pbank = [st.enter_context(nc.psum_tensor('pb%d' % i, [128, 512], F32)) for i in range(8)]

        S = Sched(nc, st)
        block = st.enter_context(nc.Block())

        class Ar:
            def __init__(self):
                self.off = 0

            def reset(self):
                self.off = 0

            def take(self, shape, dt=F32):
                esz = 4 if dt in (F32, I32) else 2
                n = int(np.prod(shape[1:]))
                nbytes = ((n * esz + 63) // 64) * 64
                assert self.off + nbytes <= ARENA, (self.off, nbytes, shape)
                a = arena[0:shape[0], self.off // 2:(self.off + n * esz) // 2]
                if dt != BF16:
                    a = a.bitcast(dt)
                self.off += nbytes
                self.peak = max(getattr(self, 'peak', 0), self.off)
                if len(shape) == 3:
                    a = a.rearrange("p (a b) -> p a b", b=shape[2])
                elif len(shape) == 4:
                    a = a.rearrange("p (a b c) -> p a b c", b=shape[2], c=shape[3])
                return a
        ar = Ar()

        psn = [0]

        def ps_next():
            i = psn[0] % 6
            psn[0] += 1
            return i, pbank[i]

        wslot = [0]

        def wload(src2d, ncols, key_extra=None):
            i = wslot[0] % 2
            wslot[0] += 1
            key = ('w', i)
            t = wr[i]
            half = ncols // 2
            if not os.environ.get('MK_NOWDMA'):
                S.dma('pool', lambda e: e.dma_start(out=t[:, 0:half], in_=src2d[:, 0:half]), writes=[(key, 0)])
                S.dma('pool', lambda e: e.dma_start(out=t[:, half:ncols], in_=src2d[:, half:ncols]), writes=[(key, 1)])
            return t, [(key, 0), (key, 1)]

        def keepw(k):
            return isinstance(k, tuple) and len(k) == 2 and isinstance(k[0], tuple) and k[0][0] == 'w'

        def phase():
            S.barrier(keep=keepw)
            ar.reset()

        def mm(out_ap, pairs, reads, pskey):
            n = len(pairs)
            fns = []
            for i, (l, r) in enumerate(pairs):
                fns.append(lambda e, l=l, r=r, i=i: e.matmul(out_ap, lhsT=l, rhs=r, start=(i == 0), stop=(i == n - 1)))
            S.op('pe', fns, reads=reads, writes=[pskey])

        S.dma('sp', lambda e: e.dma_start(out=cs[:], in_=cst), writes=['cs'])
        S.dma('sp', lambda e: e.dma_start(out=gv[:], in_=gvec), writes=['gv'])
        for l in range(DEPTH):
            S.dma('sp', lambda e, l=l: e.dma_start(out=cw[:, l, :], in_=convw[l]), writes=['cw'])
        for l in range(2):
            S.dma('sp', lambda e, l=l: e.dma_start(out=dv[:, l, :], in_=dvec[l]), writes=['dv'])
            S.dma('sp', lambda e, l=l: e.dma_start(out=bg[:, l, :], in_=bglu[l]), writes=['bg'])
        S.op('dve', lambda e: e.tensor_copy(out=identb[:], in_=cs[:, C_ID:C_ID + 128]), reads=['cs'], writes=['identb'])
        S.op('dve', lambda e: e.memset(onesb[:], 1.0), writes=['onesb'])
        S.op('dve', lambda e: e.memset(epsb[:], EPS), writes=['epsb'])
        S.op('dve', lambda e: e.memset(hpib[:], float(np.pi / 2)), writes=['hpib'])
        S.op('dve', lambda e: e.memset(halo[:], 0.0), writes=['halo'])
        S.op('dve', lambda e: e.memset(winit[:], 0.0), writes=['winit'])
        ident = cs[:, C_ID:C_ID + 128]
        JT = cs[:, C_JT:C_JT + 128]
        tri = cs[:, C_TRI:C_TRI + 128]
        iota = cs[:, C_IOTA:C_IOTA + 512]

        def sincos(ang, n, out_sin, out_cos, tmpi, tmpf, tmpr, key_in, key_s, key_c, tag):
            S.op('dve', lambda e: e.tensor_scalar(out=tmpi, in0=ang, scalar1=float(1 / TWO_PI), scalar2=None, op0=ALU.mult),
                 reads=[key_in], writes=[tag + 'i'])
            S.op('dve', lambda e: e.tensor_copy(out=tmpf, in_=tmpi), reads=[tag + 'i'], writes=[tag + 'f'])
            S.op('dve', lambda e: e.scalar_tensor_tensor(out=tmpr, in0=tmpf, scalar=-TWO_PI, in1=ang, op0=ALU.mult, op1=ALU.add),
                 reads=[tag + 'f', key_in], writes=[tag + 'r'])
            S.op('act', lambda e: e.activation(out=out_sin, in_=tmpr, func=AF.Sin), reads=[tag + 'r'], writes=[key_s])
            S.op('dve', lambda e: e.scalar_tensor_tensor(out=tmpf, in0=tmpr, scalar=-1.0, in1=tmpr, op0=ALU.mult, op1=ALU.max), reads=[tag + 'r'], writes=[tag + 'f'])
            S.op('act', lambda e: e.activation(out=out_cos, in_=tmpf, func=AF.Sin, bias=hpib[:, 0:1], scale=-1.0),
                 reads=[tag + 'f', 'hpib'], writes=[key_c])

        chk('c0')
        ar.reset()
        lm = ar.take([128, 3, NG]); dt_ = ar.take([128, NG]); ang = ar.take([128, NG])
        ti = ar.take([128, NG], I32); tf = ar.take([128, NG]); tr = ar.take([128, NG])
        sn = ar.take([128, NG]); cn = ar.take([128, NG]); nr = ar.take([128, NG]); ni = ar.take([128, NG])
        den = ar.take([128, NG]); t1 = ar.take([128, NG]); t2 = ar.take([128, NG])
        for l in range(2):
            S.dma('sp', lambda e, l=l: e.dma_start(out=lm.rearrange("p a b -> p (a b)"), in_=lam[l]), writes=['lm'])
            S.op('act', lambda e: e.activation(out=dt_, in_=lm[:, 2, :], func=AF.Exp), reads=['lm'], writes=['dt'])
            S.op('dve', lambda e: e.tensor_tensor(out=t1, in0=lm[:, 0, :], in1=dt_, op=ALU.mult), reads=['lm', 'dt'], writes=['t1'])
            S.op('act', lambda e, l=l: e.activation(out=rho[:, l, :], in_=t1, func=AF.Exp), reads=['t1'], writes=['rho'])
            S.op('dve', lambda e, l=l: e.tensor_tensor(out=th[:, l, :], in0=lm[:, 1, :], in1=dt_, op=ALU.mult), reads=['lm', 'dt'], writes=['th'])
            sincos(th[:, l, :], NG, sn, cn, ti, tf, tr, 'th', 'sn', 'cn', 'p')
            S.op('dve', lambda e, l=l: e.tensor_tensor(out=nr, in0=rho[:, l, :], in1=cn, op=ALU.mult), reads=['rho', 'cn'], writes=['nr'])
            S.op('dve', lambda e: e.tensor_scalar(out=nr, in0=nr, scalar1=-1.0, scalar2=None, op0=ALU.add), reads=['nr'], writes=['nr'])
            S.op('dve', lambda e, l=l: e.tensor_tensor(out=ni, in0=rho[:, l, :], in1=sn, op=ALU.mult), reads=['rho', 'sn'], writes=['ni'])
            S.op('dve', lambda e, l=l: e.tensor_tensor(out=A1[:, l, :], in0=rho[:, l, :], in1=cn, op=ALU.mult), reads=['rho', 'cn'], writes=['A1'])
            S.op('dve', lambda e, l=l: e.tensor_scalar(out=A2s[:, l, :], in0=ni, scalar1=cs[:, C_SGNB:C_SGNB + 1], scalar2=None, op0=ALU.mult),
                 reads=['ni', 'cs'], writes=['A2s'])
            S.op('dve', lambda e: e.tensor_tensor(out=den, in0=lm[:, 0, :], in1=lm[:, 0, :], op=ALU.mult), reads=['lm'], writes=['den'])
            S.op('dve', lambda e: e.tensor_tensor(out=t1, in0=lm[:, 1, :], in1=lm[:, 1, :], op=ALU.mult), reads=['lm'], writes=['t1'])
            S.op('dve', lambda e: e.tensor_tensor(out=den, in0=den, in1=t1, op=ALU.add), reads=['den', 't1'], writes=['den'])
            S.op('dve', lambda e: e.reciprocal(out=den, in_=den), reads=['den'], writes=['den'])
            S.op('dve', lambda e: e.tensor_tensor(out=t1, in0=nr, in1=lm[:, 0, :], op=ALU.mult), reads=['nr', 'lm'], writes=['t1'])
            S.op('dve', lambda e: e.tensor_tensor(out=t2, in0=ni, in1=lm[:, 1, :], op=ALU.mult), reads=['ni', 'lm'], writes=['t2'])
            S.op('dve', lambda e: e.tensor_tensor(out=t1, in0=t1, in1=t2, op=ALU.add), reads=['t1', 't2'], writes=['t1'])
            S.op('dve', lambda e, l=l: e.tensor_tensor(out=KR[:, l, :], in0=t1, in1=den, op=ALU.mult), reads=['t1', 'den'], writes=['KR'])
            S.op('dve', lambda e: e.tensor_tensor(out=t1, in0=ni, in1=lm[:, 0, :], op=ALU.mult), reads=['ni', 'lm'], writes=['t1'])
            S.op('dve', lambda e: e.tensor_tensor(out=t2, in0=nr, in1=lm[:, 1, :], op=ALU.mult), reads=['nr', 'lm'], writes=['t2'])
            S.op('dve', lambda e: e.tensor_tensor(out=t1, in0=t1, in1=t2, op=ALU.subtract), reads=['t1', 't2'], writes=['t1'])
            S.op('dve', lambda e, l=l: e.tensor_tensor(out=KI[:, l, :], in0=t1, in1=den, op=ALU.mult), reads=['t1', 'den'], writes=['KI'])
            S.op('dve', lambda e, l=l: e.tensor_scalar(out=KIs[:, l, :], in0=KI[:, l, :], scalar1=cs[:, C_SGNB:C_SGNB + 1], scalar2=None, op0=ALU.mult),
                 reads=['KI', 'cs'], writes=['KIs'])
            S.op('dve', lambda e, l=l: e.tensor_scalar(out=KRs[:, l, :], in0=KR[:, l, :], scalar1=cs[:, C_SGNB:C_SGNB + 1], scalar2=None, op0=ALU.mult),
                 reads=['KR', 'cs'], writes=['KRs'])
            for (mult, cc, ssn, kc, ks) in ((512.0, c512, s512, 'c512', 's512'), (511.0, c511, s511, 'c511', 's511')):
                S.op('dve', lambda e, l=l, mult=mult: e.tensor_scalar(out=ang, in0=th[:, l, :], scalar1=mult, scalar2=None, op0=ALU.mult),
                     reads=['th'], writes=['ang'])
                sincos(ang, NG, ssn[:, l, :], cc[:, l, :], ti, tf, tr, 'ang', ks, kc, 'q')

        chk('c1')
        phase()
        memT = ar.take([128, 16, NMEM])
        mh = ar.take([128, 16, NMEM], BF16)
        mtok = [ar.take([128, D]) for _ in range(2)]
        mrs = ar.take([128, NMEM])
        kvo = [ar.take([128, 512]) for _ in range(2)]
        for tb in range(2):
            S.dma('sp', lambda e, tb=tb: e.dma_start(out=mtok[tb], in_=mem[tb * 128:(tb + 1) * 128, :]), writes=[('mtok', tb)])
        for c in range(16):
            pi, pb = ps_next()
            for tb in range(2):
                S.op('pe', lambda e, c=c, tb=tb, pb=pb: e.matmul(pb[:, tb * 128:(tb + 1) * 128], lhsT=mtok[tb][:, c * 128:(c + 1) * 128],
                                                                 rhs=ident, start=True, stop=True),
                     reads=[('mtok', tb), 'cs'], writes=[('ps', pi)])
            S.op('act', lambda e, c=c, pb=pb: e.activation(out=memT[:, c, :], in_=pb[:, 0:NMEM], func=AF.Copy), reads=[('ps', pi)], writes=['memT'])
        chk('k0')
        S.op('act', lambda e: e.activation(out=mh, in_=memT, func=AF.Square), reads=['memT'], writes=['mh'])
        pi, pb = ps_next()
        mm(pb[:, 0:NMEM], [(onesb[:], mh[:, c, :]) for c in range(16)], ['onesb', 'mh'], ('ps', pi))
        S.op('act', lambda e, pb=pb: e.activation(out=mrs, in_=pb[:, 0:NMEM], func=AF.Sqrt, bias=epsb[:, 0:1], scale=1.0 / D),
             reads=[('ps', pi), 'epsb'], writes=['mrs'])
        S.op('dve', lambda e: e.reciprocal(out=mrs, in_=mrs), reads=['mrs'], writes=['mrs'])
        chk('k1')
        for l in range(DEPTH):
            for c in range(16):
                S.op('dve', lambda e, l=l, c=c: e.scalar_tensor_tensor(out=mh[:, c, :], in0=memT[:, c, :], scalar=gv[:, 128 + l * 16 + c:128 + l * 16 + c + 1],
                                                                      in1=mrs, op0=ALU.mult, op1=ALU.mult),
                     reads=['memT', 'gv', 'mrs'], writes=['mh'])
            chk('k2_%d' % l)
            wk, kk = wload(w_kv[l, 0], 16 * 512)
            wv, kv_ = wload(w_kv[l, 1], 16 * 512)
            wk3 = wk[:, :].rearrange("p (k c) -> p k c", c=512)
            wv3 = wv[:, :].rearrange("p (k c) -> p k c", c=512)
            for hd in range(4):
                pi, pb = ps_next()
                mm(pb[:, 0:NMEM], [(wk3[:, k, hd * 128:(hd + 1) * 128], mh[:, k, :]) for k in range(16)], kk + ['mh'], ('ps', pi))
                S.op('act', lambda e, l=l, hd=hd, pb=pb: e.activation(out=KT[:, l, hd, :], in_=pb[:, 0:NMEM], func=AF.Copy),
                     reads=[('ps', pi)], writes=['KT'])
            chk('k3_%d' % l)
            for which, w3, wkeys, odst in ((0, wk3, kk, o_mk), (1, wv3, kv_, o_mv)):
                for mt in range(2):
                    pi, pb = ps_next()
                    _v = os.environ.get('MK_VAR')
                    if _v == 'c':
                        mm(pb[:, 0:256], [(mh[:, k, mt * 128:(mt + 1) * 128], w3[:, k, 0:256]) for k in range(16)], wkeys + ['mh'], ('ps', pi))
                        mm(pb[:, 256:512], [(mh[:, k, mt * 128:(mt + 1) * 128], w3[:, k, 256:512]) for k in range(16)], wkeys + ['mh'], ('ps', pi))
                    elif _v in ('d', 'e', 'f', 'g', 'i'):
                        pass
                    else:
                        mm(pb[:, :], [(mh[:, k, mt * 128:(mt + 1) * 128], w3[:, k, :]) for k in range(16)], wkeys + ['mh'], ('ps', pi))
                    buf = kvo[mt]
                    if os.environ.get('MK_VAR') not in ('f', 'g'):
                        S.op('act', lambda e, pb=pb, buf=buf: e.activation(out=buf, in_=pb[:, :], func=AF.Copy), reads=[('ps', pi)], writes=[('kvo', mt)])
                    if which == 1 and os.environ.get('MK_VAR') not in ('b', 'e', 'f', 'i'):
                        if os.environ.get('MK_VAR2') == 'j':
                            S.op('act', lambda e, pb=pb, l=l, mt=mt: e.activation(out=Vt[:, l, mt, :], in_=pb[:, :], func=AF.Copy), reads=[('ps', pi)], writes=['Vt'])
                        elif os.environ.get('MK_VAR2') == 'k':
                            S.op('dve', lambda e, buf=buf, l=l, mt=mt: e.tensor_copy(out=Vt[:, l, mt, :], in_=buf), reads=[('kvo', mt)], writes=['Vt'])
                        else:
                            S.op('dve', lambda e, pb=pb, l=l, mt=mt: e.tensor_copy(out=Vt[:, l, mt, :], in_=pb[:, :]), reads=[('ps', pi)], writes=['Vt'])
                    if os.environ.get('MK_VAR') not in ('a', 'e', 'f', 'g'):
                        S.dma('sp', lambda e, buf=buf, odst=odst, l=l, mt=mt: e.dma_start(out=odst[l, mt * 128:(mt + 1) * 128, :], in_=buf),
                              reads=[('kvo', mt)])
            chk('k4_%d' % l)

        chk('c2')
        def rmsnorm_to_h(goff, tiles, out_fp32=None):
            for (t0, tn) in tiles:
                S.op('act', lambda e, t0=t0, tn=tn: e.activation(out=h[:, :, t0:t0 + tn], in_=x[:, :, t0:t0 + tn], func=AF.Square),
                     reads=['x'], writes=['h'])
                pi, pb = ps_next()
                mm(pb[:, 0:tn], [(onesb[:], h[:, c, t0:t0 + tn]) for c in range(16)], ['onesb', 'h'], ('ps', pi))
                S.op('act', lambda e, pb=pb, t0=t0, tn=tn: e.activation(out=rsd[:, t0:t0 + tn], in_=pb[:, 0:tn], func=AF.Sqrt,
                                                                        bias=epsb[:, 0:1], scale=1.0 / D),
                     reads=[('ps', pi), 'epsb'], writes=['rsd'])
                S.op('dve', lambda e, t0=t0, tn=tn: e.reciprocal(out=rstd[:, t0:t0 + tn], in_=rsd[:, t0:t0 + tn]), reads=['rsd'], writes=['rstd'])
                for c in range(16):
                    dst = h if out_fp32 is None else out_fp32
                    S.op('dve', lambda e, c=c, t0=t0, tn=tn, dst=dst: e.scalar_tensor_tensor(
                        out=dst[:, c, t0:t0 + tn], in0=x[:, c, t0:t0 + tn], scalar=gv[:, goff + c:goff + c + 1],
                        in1=rstd[:, t0:t0 + tn], op0=ALU.mult, op1=ALU.mult),
                        reads=['x', 'gv', 'rstd'], writes=['h' if out_fp32 is None else 'yout'])

        def xattn_prompt(l, q, mix, pT, rz):
            sc = float(128 ** -0.5)
            for hd in range(4):
                for mt in range(2):
                    pi, pb = ps_next()
                    mm(pb[:, :], [(KT[:, l, hd, mt * 128:(mt + 1) * 128], q[:, hd, 0:TP])], ['KT', 'q'], ('ps', pi))
                    S.op('act', lambda e, pb=pb, mt=mt: e.activation(out=pT[:, mt, :], in_=pb[:, :], func=AF.Exp, scale=sc),
                         reads=[('ps', pi)], writes=[('pT', mt)])
                pi, pbz = ps_next()
                mm(pbz[:, :], [(onesb[:], pT[:, mt, :]) for mt in range(2)], ['onesb', ('pT', 0), ('pT', 1)], ('ps', pi))
                S.op('dve', lambda e, pbz=pbz: e.reciprocal(out=rz, in_=pbz[:, :]), reads=[('ps', pi)], writes=['rz'])
                pi, pbo = ps_next()
                mm(pbo[:, :], [(Vt[:, l, mt, hd * 128:(hd + 1) * 128], pT[:, mt, :]) for mt in range(2)], ['Vt', ('pT', 0), ('pT', 1)], ('ps', pi))
                S.op('dve', lambda e, pbo=pbo, hd=hd: e.tensor_tensor(out=mix[:, 12 + hd, 0:TP], in0=pbo[:, :], in1=rz, op=ALU.mult),
                     reads=[('ps', pi), 'rz'], writes=['h'])

        def xattn_sample(l, wq3, wqkeys, mix, abuf):
            sc = float(128 ** -0.5)
            qtok, qm, kb, vb, prod, scr, E, rzs = abuf['qtok'], abuf['qm'], abuf['kb'], abuf['vb'], abuf['prod'], abuf['scr'], abuf['E'], abuf['rzs']
            pi, pb = ps_next()
            mm(pb[0:SS, :], [(h[:, k, TP:TT], wq3[:, k, :]) for k in range(16)], wqkeys + ['h'], ('ps', pi))
            S.op('act', lambda e, pb=pb: e.activation(out=qtok, in_=pb[0:SS, :], func=AF.Copy), reads=[('ps', pi)], writes=['qtok'])
            for b in range(SS):
                S.dma('pool', lambda e, b=b: e.dma_start(out=kb, in_=ck[l, b].rearrange("(mt m) f -> m mt f", m=128)), writes=['kb'])
                S.op('dve', lambda e, b=b: e.tensor_scalar(out=qm, in0=qtok, scalar1=cs[0:SS, C_ID + b:C_ID + b + 1], scalar2=None, op0=ALU.mult),
                     reads=['qtok', 'cs'], writes=['qm'])
                pi, pbq = ps_next()
                mm(pbq[:, :], [(onesb[0:SS, :], qm)], ['onesb', 'qm'], ('ps', pi))
                for mt in range(2):
                    S.op('dve', lambda e, mt=mt, pbq=pbq: e.tensor_tensor(out=prod, in0=kb[:, mt, :], in1=pbq[:, :], op=ALU.mult),
                         reads=['kb', ('ps', pi)], writes=['prod'])
                    S.op('dve', lambda e, b=b, mt=mt: e.tensor_reduce(out=scr[:, b, mt, :], in_=prod.rearrange("p (h d) -> p h d", d=128),
                                                                      axis=AX.X, op=ALU.add),
                         reads=['prod'], writes=['scr'])
            S.op('act', lambda e: e.activation(out=E, in_=scr, func=AF.Exp, scale=sc), reads=['scr'], writes=['E'])
            pi, pbz = ps_next()
            mm(pbz[:, 0:128], [(onesb[:], E.rearrange("p b m h -> p (b m h)"))], ['onesb', 'E'], ('ps', pi))
            zv = pbz[:, 0:128].rearrange("p (b m h) -> p b m h", m=2, h=4)
            S.op('dve', lambda e, zv=zv: e.tensor_copy(out=rzs, in_=zv[:, :, 0, :]), reads=[('ps', pi)], writes=['rzs'])
            S.op('dve', lambda e, zv=zv: e.tensor_tensor(out=rzs, in0=rzs, in1=zv[:, :, 1, :], op=ALU.add), reads=[('ps', pi), 'rzs'], writes=['rzs'])
            S.op('dve', lambda e: e.reciprocal(out=rzs, in_=rzs), reads=['rzs'], writes=['rzs'])
            pbo = pbank[7]
            for b in range(SS):
                S.dma('pool', lambda e, b=b: e.dma_start(out=vb, in_=cv[l, b].rearrange("(mt m) f -> m mt f", m=128)), writes=['vb'])
                for hd in range(4):
                    mm(pbo[:, hd * SS + b:hd * SS + b + 1],
                       [(vb[:, mt, hd * 128:(hd + 1) * 128], E[:, b, mt, hd:hd + 1]) for mt in range(2)], ['vb', 'E'], ('ps', 7))
            for hd in range(4):
                S.op('dve', lambda e, hd=hd: e.tensor_tensor(out=mix[:, 12 + hd, TP:TT], in0=pbo[:, hd * SS:(hd + 1) * SS], in1=rzs[:, :, hd], op=ALU.mult),
                     reads=[('ps', 7), 'rzs'], writes=['h'])

        def xattn_bufs():
            return dict(qtok=ar.take([SS, 512], BF16), qm=ar.take([SS, 512], BF16), kb=ar.take([128, 2, 512], BF16),
                        vb=ar.take([128, 2, 512], BF16), prod=ar.take([128, 512]),
                        scr=ar.take([128, SS, 2, 4]), E=ar.take([128, SS, 2, 4], BF16), rzs=ar.take([128, SS, 4]))

        def subphase(off):
            S.barrier(keep=keepw)
            ar.off = off

        def out_proj(wsrc, j2, mix, tiles):
            for j in range(4):
                wt, wkeys = wload(wsrc[j2, j], 16 * 512)
                w3 = wt[:, :].rearrange("p (k c) -> p k c", c=512)
                for mi in range(4):
                    m = 4 * j + mi
                    for (t0, tn) in tiles:
                        pi, pb = ps_next()
                        mm(pb[:, 0:tn], [(w3[:, k, mi * 128:(mi + 1) * 128], mix[:, k, t0:t0 + tn]) for k in range(16)], wkeys + ['h'], ('ps', pi))
                        S.op('dve', lambda e, pb=pb, m=m, t0=t0, tn=tn: e.tensor_tensor(out=x[:, m, t0:t0 + tn], in0=x[:, m, t0:t0 + tn], in1=pb[:, 0:tn], op=ALU.add),
                             reads=[('ps', pi), 'x'], writes=['x'])

        for p in range(NPASS_):
            last = (p == NPASS_ - 1)
            tiles = [(0, TP)] + ([(TP, SS)] if last else [])
            phase()
            xt = [ar.take([128, D]) for _ in range(2)]
            nblk = 4 + (1 if last else 0)
            for tb in range(nblk):
                rows = 128 if tb < 4 else SS
                src = xp[p * TP + tb * 128:p * TP + tb * 128 + 128, :] if tb < 4 else xs
                buf = xt[tb % 2]
                S.dma('sp', lambda e, buf=buf, src=src, rows=rows: e.dma_start(out=buf[0:rows, :], in_=src), writes=[('xt', tb % 2)])
                for cg in range(4):
                    pi, pb = ps_next()
                    for ci in range(4):
                        c = cg * 4 + ci
                        S.op('pe', lambda e, buf=buf, rows=rows, c=c, ci=ci, pb=pb: e.matmul(
                            pb[:, ci * 128:ci * 128 + rows], lhsT=buf[0:rows, c * 128:(c + 1) * 128], rhs=ident[0:rows, 0:rows], start=True, stop=True),
                            reads=[('xt', tb % 2), 'cs'], writes=[('ps', pi)])
                    S.op('act', lambda e, pb=pb, cg=cg, tb=tb, rows=rows: e.activation(
                        out=x[:, cg * 4:cg * 4 + 4, tb * 128:tb * 128 + rows],
                        in_=pb[:, :].rearrange("p (c t) -> p c t", t=128)[:, :, 0:rows], func=AF.Copy),
                        reads=[('ps', pi)], writes=['x'])

            chk('c3')
            for l in range(DEPTH):
                j2 = l // 2
                phase()
                rmsnorm_to_h(l * 16, tiles)
                chk('c4')
                mix = h
                u = ar.take([128, 12, TT])
                dbg_u[0] = u
                if l % 2 == 0:
                    vtb = ar.take([128, 4, TW], BF16)
                    vts = ar.take([SS, TW]); vnb = ar.take([SS, TW], BF16)
                    ssq = ar.take([128, 5, 4])
                    off0 = ar.off
                    q = ar.take([128, 4, TT], BF16); pT = ar.take([128, 2, TP], BF16); rz = ar.take([128, TP])
                    vtmp = [ar.take([128, 512]) for _ in range(2)]; vsq = ar.take([128, 512])
                    if last:
                        xb_ = xattn_bufs()
                    nv = 0
                    for j in range(7):
                        wt, wkeys = wload(w_sgin[j2, j], 16 * 512)
                        w3 = wt[:, :].rearrange("p (k c) -> p k c", c=512)
                        if j < 3 or j == 6:
                            for mi in range(4):
                                for (t0, tn) in tiles:
                                    pi, pb = ps_next()
                                    mm(pb[:, 0:tn], [(w3[:, k, mi * 128:(mi + 1) * 128], h[:, k, t0:t0 + tn]) for k in range(16)], wkeys + ['h'], ('ps', pi))
                                    if j < 3:
                                        S.op('act', lambda e, pb=pb, m=4 * j + mi, t0=t0, tn=tn: e.activation(out=u[:, m, t0:t0 + tn], in_=pb[:, 0:tn], func=AF.Gelu_apprx_tanh),
                                             reads=[('ps', pi)], writes=['u'])
                                    else:
                                        S.op('act', lambda e, pb=pb, mi=mi, t0=t0, tn=tn: e.activation(out=q[:, mi, t0:t0 + tn], in_=pb[:, 0:tn], func=AF.Copy),
                                             reads=[('ps', pi)], writes=['q'])
                            if j == 6 and last:
                                xattn_sample(l, w3, wkeys, mix, xb_)
                        else:
                            for tb in range(nblk):
                                rows = 128 if tb < 4 else SS
                                pi, pb = ps_next()
                                mm(pb[0:rows, :], [(h[:, k, tb * 128:tb * 128 + rows], w3[:, k, :]) for k in range(16)], wkeys + ['h'], ('ps', pi))
                                vt_ = vtmp[nv % 2]; kv = ('vtmp', nv % 2); nv += 1
                                cols = slice((j - 3) * 512, (j - 2) * 512)
                                if tb < 4:
                                    S.op('act', lambda e, pb=pb, vt_=vt_: e.activation(out=vt_, in_=pb[:, :], func=AF.Gelu_apprx_tanh), reads=[('ps', pi)], writes=[kv])
                                    S.op('dve', lambda e, vt_=vt_, tb=tb, cols=cols: e.tensor_copy(out=vtb[:, tb, cols], in_=vt_), reads=[kv], writes=['vtb'])
                                    S.op('act', lambda e, vt_=vt_: e.activation(out=vsq, in_=vt_, func=AF.Square), reads=[kv], writes=['vsq'])
                                    S.op('dve', lambda e, tb=tb, j=j: e.tensor_reduce(out=ssq[:, tb, j - 3:j - 2], in_=vsq, axis=AX.X, op=ALU.add), reads=['vsq'], writes=['ssq'])
                                else:
                                    S.op('act', lambda e, pb=pb, cols=cols: e.activation(out=vts[:, cols], in_=pb[0:SS, :], func=AF.Gelu_apprx_tanh), reads=[('ps', pi)], writes=['vts'])
                                    S.op('act', lambda e, cols=cols: e.activation(out=vsq[0:SS, :], in_=vts[:, cols], func=AF.Square), reads=['vts'], writes=['vsq'])
                                    S.op('dve', lambda e, tb=tb, j=j: e.tensor_reduce(out=ssq[0:SS, tb, j - 3:j - 2], in_=vsq[0:SS, :], axis=AX.X, op=ALU.add), reads=['vsq'], writes=['ssq'])
                    chk('c5a')
                    xattn_prompt(l, q, mix, pT, rz)
                    chk('c5')
                    subphase(off0)
                    gvb = ar.take([128, TW]); bsb = ar.take([128, 12, 128]); wsb = ar.take([128, 12, 128], BF16)
                    w00t = ar.take([SS, 12]); dg = ar.take([SS, 12, SS], BF16); tmpg = ar.take([128, TP])
                    S.dma('sp', lambda e: e.dma_start(out=gvb, in_=gv_bc[j2]), writes=['gvb'])
                    S.dma('sp', lambda e: e.dma_start(out=bsb.rearrange("p a b -> p (a b)"), in_=bs_bc[j2]), writes=['bsb'])
                    S.dma('pool', lambda e: e.dma_start(out=wsb.rearrange("p a b -> p (a b)"), in_=wsT[j2]), writes=['wsb'])
                    for g in range(12):
                        S.op('dve', lambda e, g=g: e.tensor_tensor(out=wsb[:, g, :], in0=wsb[:, g, :], in1=tri, op=ALU.mult), reads=['wsb', 'cs'], writes=['wsb'])
                    if last:
                        S.dma('sp', lambda e: e.dma_start(out=w00t, in_=w00[j2]), writes=['w00t'])
                        for g in range(12):
                            S.op('dve', lambda e, g=g: e.tensor_scalar(out=dg[:, g, :], in0=ident[0:SS, 0:SS], scalar1=w00t[:, g:g + 1], scalar2=None, op0=ALU.mult),
                                 reads=['w00t', 'cs'], writes=['dg'])
                    for tb in range(nblk):
                        rows = 128 if tb < 4 else SS
                        S.op('dve', lambda e, tb=tb, rows=rows: e.tensor_reduce(out=ssq[0:rows, tb, 3:4], in_=ssq[0:rows, tb, 0:3], axis=AX.X, op=ALU.add), reads=['ssq'], writes=['ssq'])
                        S.op('act', lambda e, tb=tb, rows=rows: e.activation(out=ssq[0:rows, tb, 3:4], in_=ssq[0:rows, tb, 3:4], func=AF.Sqrt, bias=epsb[0:rows, 0:1], scale=1.0 / TW),
                             reads=['ssq', 'epsb'], writes=['ssq'])
                        S.op('dve', lambda e, tb=tb, rows=rows: e.reciprocal(out=ssq[0:rows, tb, 3:4], in_=ssq[0:rows, tb, 3:4]), reads=['ssq'], writes=['ssq'])
                        if tb < 4:
                            S.op('dve', lambda e, tb=tb: e.scalar_tensor_tensor(out=vtb[:, tb, :], in0=vtb[:, tb, :], scalar=ssq[:, tb, 3:4], in1=gvb, op0=ALU.mult, op1=ALU.mult),
                                 reads=['vtb', 'ssq', 'gvb'], writes=['vtb'])
                        else:
                            S.op('dve', lambda e, tb=tb: e.scalar_tensor_tensor(out=vts, in0=vts, scalar=ssq[0:SS, tb, 3:4], in1=gvb[0:SS, :], op0=ALU.mult, op1=ALU.mult),
                                 reads=['vts', 'ssq', 'gvb'], writes=['vts'])
                            S.op('dve', lambda e: e.tensor_copy(out=vnb, in_=vts), reads=['vts'], writes=['vnb'])
                            S.dma('sp', lambda e: e.dma_start(out=o_sgv[j2], in_=vts), reads=['vts'])
                    for g in range(12):
                        pi, pb = ps_next()
                        for tb in range(4):
                            S.op('pe', lambda e, g=g, tb=tb, pb=pb: e.matmul(pb[:, tb * 128:(tb + 1) * 128], lhsT=vtb[:, tb, g * 128:(g + 1) * 128], rhs=wsb[:, g, :], start=True, stop=True),
                                 reads=['vtb', 'wsb'], writes=[('ps', pi)])
                        for tb in range(4):
                            S.op('dve', lambda e, g=g, tb=tb, pb=pb: e.tensor_tensor(out=tmpg[:, tb * 128:(tb + 1) * 128], in0=pb[:, tb * 128:(tb + 1) * 128], in1=bsb[:, g, :], op=ALU.add),
                                 reads=[('ps', pi), 'bsb'], writes=['tmpg'])
                        S.op('dve', lambda e, g=g: e.tensor_tensor(out=mix[:, g, 0:TP], in0=tmpg, in1=u[:, g, 0:TP], op=ALU.mult), reads=['tmpg', 'u'], writes=['h'])
                        if last:
                            pi, pb = ps_next()
                            mm(pb[:, 0:SS], [(vnb[:, g * 128:(g + 1) * 128], dg[:, g, :])], ['vnb', 'dg'], ('ps', pi))
                            S.op('dve', lambda e, g=g, pb=pb: e.scalar_tensor_tensor(out=mix[:, g, TP:TT], in0=pb[:, 0:SS], scalar=bsb[:, g, 0:1], in1=u[:, g, TP:TT], op0=ALU.add, op1=ALU.mult),
                                 reads=[('ps', pi), 'bsb', 'u'], writes=['h'])
                    if STOP == 'c6v':
                        S.dma('pool', lambda e: e.dma_start(out=o_dbgh[:, 0:4 * TW], in_=vtb.rearrange("p a b -> p (a b)")), reads=['vtb'])
                        S.dma('sp', lambda e: e.dma_start(out=o_dbgu[:, 0:20], in_=ssq.rearrange("p a b -> p (a b)")), reads=['ssq'])
                        S.dma('sp', lambda e: e.dma_start(out=o_dbgx[0:SS, 0:TW], in_=vts), reads=['vts'])
                        S.dma('pool', lambda e: e.dma_start(out=o_dbgx[:, 2048:2048 + 12 * 128], in_=wsb.rearrange("p a b -> p (a b)")), reads=['wsb'])
                        S.stopped = True
                    chk('c6')
                    out_proj(w_sgout, j2, mix, tiles)
                    chk('c7')
                else:
                    ub = ar.take([128, 12, TT], BF16)
                    off0 = ar.off
                    q = ar.take([128, 4, TT], BF16); pT = ar.take([128, 2, TP], BF16); rz = ar.take([128, TP])
                    if last:
                        xb_ = xattn_bufs()
                    for j in range(4):
                        wt, wkeys = wload(w_ssin[j2, j], 16 * 512)
                        w3 = wt[:, :].rearrange("p (k c) -> p k c", c=512)
                        for mi in range(4):
                            for (t0, tn) in tiles:
                                pi, pb = ps_next()
                                mm(pb[:, 0:tn], [(w3[:, k, mi * 128:(mi + 1) * 128], h[:, k, t0:t0 + tn]) for k in range(16)], wkeys + ['h'], ('ps', pi))
                                if j < 3:
                                    m = 4 * j + mi
                                    S.op('act', lambda e, pb=pb, m=m, t0=t0, tn=tn: e.activation(out=u[:, m, t0:t0 + tn], in_=pb[:, 0:tn], func=AF.Copy), reads=[('ps', pi)], writes=[('u', m)])
                                    S.op('dve', lambda e, pb=pb, m=m, t0=t0, tn=tn: e.tensor_copy(out=ub[:, m, t0:t0 + tn], in_=pb[:, 0:tn]), reads=[('ps', pi)], writes=[('ub', m)])
                                else:
                                    S.op('act', lambda e, pb=pb, mi=mi, t0=t0, tn=tn: e.activation(out=q[:, mi, t0:t0 + tn], in_=pb[:, 0:tn], func=AF.Copy), reads=[('ps', pi)], writes=['q'])
                        if j == 3 and last:
                            xattn_sample(l, w3, wkeys, mix, xb_)
                    chk('c9')
                    xattn_prompt(l, q, mix, pT, rz)
                    subphase(off0)
                    bshape = [128, 8, 16]
                    Bc = ar.take(bshape); Bs = ar.take(bshape); Bt = ar.take(bshape); Bt2 = ar.take(bshape); tmpB = ar.take(bshape)
                    Cc = ar.take(bshape); Cs_ = ar.take(bshape)
                    Bpad = [ar.take([128, 8, 128], BF16) for _ in range(2)]
                    Cpad = [ar.take([128, 8 * 144], BF16) for _ in range(2)]
                    Tm = ar.take([128, 128])
                    angt = ar.take([128, TP]); ki = ar.take([128, TP], I32); kf = ar.take([128, TP])
                    SgB = [ar.take([128, TP]) for _ in range(2)]; CgB = [ar.take([128, TP]) for _ in range(2)]
                    ta = ar.take([128, TP]); tb_ = ar.take([128, TP]); W = ar.take([128, TP])
                    G1 = ar.take([128, TP], BF16); G2 = ar.take([128, TP], BF16)
                    Wl = ar.take([128, NG]); JW = ar.take([128, NG]); tcar = ar.take([128, NG]); hfin = ar.take([128, NG])
                    if last:
                        s0T = ar.take([128, 8, SS]); s0sT = ar.take([128, 8, SS]); hs = ar.take([128, 8, SS]); hsb = ar.take([128, 8, SS], BF16)
                        stok1 = ar.take([SS, 8 * 128]); stok = [stok1, stok1]
                        hst = ar.take([SS, 4 * 128])
                    S.op('dve', lambda e: e.memset(Cpad[0], 0.0), writes=[('Cpad', 0)])
                    S.op('dve', lambda e: e.memset(Cpad[1], 0.0), writes=[('Cpad', 1)])
                    def stage1(g):
                        Sg = SgB[g % 2]; Cg = CgB[g % 2]; kS = ('Sg', g % 2); kC = ('Cg', g % 2)
                        S.op('dve', lambda e: e.tensor_scalar(out=angt, in0=iota, scalar1=th[:, j2, g:g + 1], scalar2=None, op0=ALU.mult), reads=['cs', 'th'], writes=['angt'])
                        S.op('dve', lambda e: e.tensor_scalar(out=ki, in0=angt, scalar1=float(1 / TWO_PI), scalar2=None, op0=ALU.mult), reads=['angt'], writes=['ki'])
                        S.op('dve', lambda e: e.tensor_copy(out=kf, in_=ki), reads=['ki'], writes=['kf'])
                        S.op('dve', lambda e: e.scalar_tensor_tensor(out=angt, in0=kf, scalar=-TWO_PI, in1=angt, op0=ALU.mult, op1=ALU.add), reads=['kf', 'angt'], writes=['angt'])
                        S.op('act', lambda e: e.activation(out=Sg, in_=angt, func=AF.Sin, scale=-1.0), reads=['angt'], writes=[kS])
                        S.op('dve', lambda e: e.scalar_tensor_tensor(out=kf, in0=angt, scalar=-1.0, in1=angt, op0=ALU.mult, op1=ALU.max), reads=['angt'], writes=['kf'])
                        S.op('act', lambda e: e.activation(out=Cg, in_=kf, func=AF.Sin, bias=hpib[:, 0:1], scale=-1.0), reads=['kf', 'hpib'], writes=[kC])

                    for j8 in range(12):
                        gs = slice(j8 * 128, (j8 + 1) * 128)
                        S.dma('sp', lambda e, gs=gs: e.dma_start(out=Bc.rearrange("p a b -> p (a b)"), in_=Bcat[j2][:, gs]), writes=['Bc'])
                        S.dma('sp', lambda e, gs=gs: e.dma_start(out=Bs.rearrange("p a b -> p (a b)"), in_=Bsw[j2][:, gs]), writes=['Bs'])
                        S.dma('sp', lambda e, gs=gs: e.dma_start(out=Cc.rearrange("p a b -> p (a b)"), in_=Ccat[j2][:, gs]), writes=['Cc'])
                        S.dma('sp', lambda e, gs=gs: e.dma_start(out=Cs_.rearrange("p a b -> p (a b)"), in_=Csw[j2][:, gs]), writes=['Cs'])
                        g8 = slice(j8 * 8, (j8 + 1) * 8)
                        S.op('dve', lambda e, g8=g8: e.tensor_tensor(out=Bt, in0=Bc, in1=KR[:, j2, g8].unsqueeze(2).to_broadcast(bshape), op=ALU.mult), reads=['Bc', 'KR'], writes=['Bt'])
                        S.op('dve', lambda e, g8=g8: e.tensor_tensor(out=tmpB, in0=Bs, in1=KIs[:, j2, g8].unsqueeze(2).to_broadcast(bshape), op=ALU.mult), reads=['Bs', 'KIs'], writes=['tmpB'])
                        S.op('dve', lambda e: e.tensor_tensor(out=Bt, in0=Bt, in1=tmpB, op=ALU.add), reads=['Bt', 'tmpB'], writes=['Bt'])
                        S.op('dve', lambda e, g8=g8: e.tensor_tensor(out=Bt2, in0=Bs, in1=KRs[:, j2, g8].unsqueeze(2).to_broadcast(bshape), op=ALU.mult), reads=['Bs', 'KRs'], writes=['Bt2'])
                        S.op('dve', lambda e, g8=g8: e.tensor_tensor(out=tmpB, in0=Bc, in1=KI[:, j2, g8].unsqueeze(2).to_broadcast(bshape), op=ALU.mult), reads=['Bc', 'KI'], writes=['tmpB'])
                        S.op('dve', lambda e: e.tensor_tensor(out=Bt2, in0=Bt2, in1=tmpB, op=ALU.subtract), reads=['Bt2', 'tmpB'], writes=['Bt2'])
                        for wi, Bsrc in ((0, Bt), (1, Bt2)):
                            pi, pb = ps_next()
                            mm(pb[:, 0:128], [(Bsrc.rearrange("p a b -> p (a b)"), ident)], ['Bt' if wi == 0 else 'Bt2', 'cs'], ('ps', pi))
                            S.op('act', lambda e, pb=pb: e.activation(out=Tm, in_=pb[:, 0:128], func=AF.Copy), reads=[('ps', pi)], writes=['Tm'])
                            for i in range(8):
                                S.op('dve', lambda e, wi=wi, i=i: e.tensor_scalar(out=Bpad[wi][:, i, :], in0=Tm, scalar1=cs[:, C_GMASK + i:C_GMASK + i + 1], scalar2=None, op0=ALU.mult),
                                     reads=['Tm', 'cs'], writes=[('Bpad', wi)])
                        for wi, Csrc in ((0, Cc), (1, Cs_)):
                            S.op('dve', lambda e, wi=wi, Csrc=Csrc: e.tensor_copy(out=Cpad[wi].rearrange("p (i s) -> p i s", s=144)[:, :, 0:16], in_=Csrc),
                                 reads=['Cc' if wi == 0 else 'Cs'], writes=[('Cpad', wi)])
                        if last:
                            for nm, src, dstT in (('cat', ss_cat, s0T), ('sw', ss_sw, s0sT)):
                                sk = stok[0 if nm == 'cat' else 1]
                                S.dma('sp', lambda e, src=src, sk=sk: e.dma_start(out=sk, in_=src[j2][:, j8 * 1024:(j8 + 1) * 1024]), writes=['stok'])
                                pi, pb = ps_next()
                                for gi in range(8):
                                    S.op('pe', lambda e, gi=gi, pb=pb, sk=sk: e.matmul(pb[:, gi * SS:(gi + 1) * SS], lhsT=sk[:, gi * 128:(gi + 1) * 128], rhs=ident[0:SS, 0:SS], start=True, stop=True),
                                         reads=['stok', 'cs'], writes=[('ps', pi)])
                                S.op('act', lambda e, pb=pb, dstT=dstT: e.activation(out=dstT, in_=pb[:, 0:8 * SS].rearrange("p (g b) -> p g b", b=SS), func=AF.Copy),
                                     reads=[('ps', pi)], writes=['s0' + nm])
                        pby = pbank[6]; pbys = pbank[7]
                        for i in range(8):
                            g = j8 * 8 + i
                            if i == 0 and j8 == 0:
                                stage1(0)
                            if g + 1 < NG:
                                stage1(g + 1)
                            Sg = SgB[g % 2]; Cg = CgB[g % 2]; kS = ('Sg', g % 2); kC = ('Cg', g % 2)
                            pi0, pb0 = ps_next()
                            mm(pb0[:, :], [(Bpad[0][:, i, :], ub[:, j8, 0:TP])], [('Bpad', 0), ('ub', j8)], ('ps', pi0))
                            pi1, pb1 = ps_next()
                            mm(pb1[:, :], [(Bpad[1][:, i, :], ub[:, j8, 0:TP])], [('Bpad', 1), ('ub', j8)], ('ps', pi1))
                            S.op('dve', lambda e, pb0=pb0: e.tensor_tensor(out=ta, in0=pb0[:, :], in1=Cg, op=ALU.mult), reads=[('ps', pi0), kC], writes=['ta'])
                            S.op('dve', lambda e, pb1=pb1: e.tensor_tensor(out=tb_, in0=pb1[:, :], in1=Sg, op=ALU.mult), reads=[('ps', pi1), kS], writes=['tb'])
                            S.op('pool', lambda e: e.tensor_tensor(out=ta, in0=ta, in1=tb_, op=ALU.add), reads=['ta', 'tb'], writes=['ta'])
                            S.op('dve', lambda e, g=g: e.tensor_tensor_scan(out=W, data0=rho[:, j2, g:g + 1].to_broadcast([128, TP]), data1=ta, initial=winit[:, j2, g:g + 1], op0=ALU.mult, op1=ALU.add),
                                 reads=['ta', 'rho', 'winit'], writes=['W'])
                            S.op('dve', lambda e: e.scalar_tensor_tensor(out=G1, in0=W, scalar=cs[:, C_SGNC:C_SGNC + 1], in1=Cg, op0=ALU.mult, op1=ALU.mult), reads=['W', kC, 'cs'], writes=['G1'])
                            S.op('pool', lambda e: e.tensor_tensor(out=G2, in0=W, in1=Sg, op=ALU.mult), reads=['W', kS], writes=['G2'])
                            S.op('act', lambda e, g=g: e.activation(out=Wl[:, g:g + 1], in_=W[:, TP - 1:TP], func=AF.Copy), reads=['W'], writes=['Wl'])
                            fns = [lambda e, i=i: e.matmul(pby[:, :], lhsT=Cpad[0][:, i * 128:(i + 1) * 128], rhs=G1, start=(i == 0), stop=False),
                                   lambda e, i=i: e.matmul(pby[:, :], lhsT=Cpad[1][:, i * 128:(i + 1) * 128], rhs=G2, start=False, stop=(i == 7))]
                            S.op('pe', fns, reads=[('Cpad', 0), ('Cpad', 1), 'G1', 'G2'], writes=[('ps', 6)])
                            if last:
                                pis, pbs = ps_next()
                                mm(pbs[:, 0:SS], [(Bpad[0][:, i, :], ub[:, j8, TP:TT])], [('Bpad', 0), ('ub', j8)], ('ps', pis))
                                S.op('dve', lambda e, g=g, i=i, pbs=pbs: e.scalar_tensor_tensor(out=hs[:, i, :], in0=s0T[:, i, :], scalar=A1[:, j2, g:g + 1], in1=pbs[:, 0:SS], op0=ALU.mult, op1=ALU.add),
                                     reads=['s0cat', 'A1', ('ps', pis)], writes=['hs'])
                                S.op('dve', lambda e, g=g, i=i: e.scalar_tensor_tensor(out=hs[:, i, :], in0=s0sT[:, i, :], scalar=A2s[:, j2, g:g + 1], in1=hs[:, i, :], op0=ALU.mult, op1=ALU.add),
                                     reads=['s0sw', 'A2s', 'hs'], writes=['hs'])
                                S.op('dve', lambda e, i=i: e.tensor_scalar(out=hsb[:, i, :], in0=hs[:, i, :], scalar1=cs[:, C_SGNC:C_SGNC + 1], scalar2=None, op0=ALU.mult),
                                     reads=['hs', 'cs'], writes=['hsb'])
                                S.op('pe', lambda e, i=i: e.matmul(pbys[:, 0:SS], lhsT=Cpad[0][:, i * 128:(i + 1) * 128], rhs=hsb[:, i, :], start=(i == 0), stop=(i == 7)),
                                     reads=[('Cpad', 0), 'hsb'], writes=[('ps', 7)])
                        if last:
                            for half in range(2):
                                pi, pb = ps_next()
                                for gi in range(4):
                                    S.op('pe', lambda e, gi=gi, half=half, pb=pb: e.matmul(pb[0:SS, gi * 128:(gi + 1) * 128], lhsT=hs[:, half * 4 + gi, :], rhs=ident, start=True, stop=True),
                                         reads=['hs', 'cs'], writes=[('ps', pi)])
                                S.op('act', lambda e, pb=pb, half=half: e.activation(out=hst, in_=pb[0:SS, :], func=AF.Copy), reads=[('ps', pi)], writes=['hst'])
                                S.dma('sp', lambda e, j8=j8, half=half: e.dma_start(out=o_ssm_s[j2][:, j8 * 1024 + half * 512:j8 * 1024 + (half + 1) * 512], in_=hst), reads=['hst'])
                        ytl = [(0, TP, pby, 6)] + ([(TP, SS, pbys, 7)] if last else [])
                        for (t0, tn, pbb, pii) in ytl:
                            S.op('dve', lambda e, t0=t0, tn=tn, pbb=pbb: e.scalar_tensor_tensor(out=u[:, j8, t0:t0 + tn], in0=u[:, j8, t0:t0 + tn], scalar=dv[:, j2, j8:j8 + 1], in1=pbb[:, 0:tn], op0=ALU.mult, op1=ALU.add),
                                 reads=[('u', j8), 'dv', ('ps', pii)], writes=[('u', j8)])
                            S.op('act', lambda e, t0=t0, tn=tn: e.activation(out=u[:, j8, t0:t0 + tn], in_=u[:, j8, t0:t0 + tn], func=AF.Gelu_apprx_tanh), reads=[('u', j8)], writes=[('u', j8)])
                            S.op('dve', lambda e, t0=t0, tn=tn: e.tensor_copy(out=ub[:, j8, t0:t0 + tn], in_=u[:, j8, t0:t0 + tn]), reads=[('u', j8)], writes=[('ub', j8)])
                    chk('c10')
                    pi, pb = ps_next()
                    mm(pb[:, 0:NG], [(JT, Wl)], ['cs', 'Wl'], ('ps', pi))
                    S.op('act', lambda e, pb=pb: e.activation(out=JW, in_=pb[:, 0:NG], func=AF.Copy), reads=[('ps', pi)], writes=['JW'])
                    if last:
                        S.op('dve', lambda e: e.tensor_tensor(out=hfin, in0=Wl, in1=c511[:, j2, :], op=ALU.mult), reads=['Wl', 'c511'], writes=['hfin'])
                        S.op('dve', lambda e: e.tensor_tensor(out=tcar, in0=JW, in1=s511[:, j2, :], op=ALU.mult), reads=['JW', 's511'], writes=['tcar'])
                        S.op('dve', lambda e: e.tensor_tensor(out=hfin, in0=hfin, in1=tcar, op=ALU.add), reads=['hfin', 'tcar'], writes=['hfin'])
                        S.dma('sp', lambda e: e.dma_start(out=o_ssm_p[j2], in_=hfin), reads=['hfin'])
                    else:
                        S.op('dve', lambda e: e.tensor_tensor(out=tcar, in0=JW, in1=s512[:, j2, :], op=ALU.mult), reads=['JW', 's512'], writes=['tcar'])
                        S.op('dve', lambda e: e.tensor_tensor(out=hfin, in0=Wl, in1=c512[:, j2, :], op=ALU.mult), reads=['Wl', 'c512'], writes=['hfin'])
                        S.op('dve', lambda e: e.tensor_tensor(out=winit[:, j2, :], in0=hfin, in1=tcar, op=ALU.add), reads=['hfin', 'tcar', 'W'], writes=['winit'])
                    for j in range(3):
                        wt, wkeys = wload(w_glu[j2, j], 12 * 512)
                        w3 = wt[:, 0:12 * 512].rearrange("p (k c) -> p k c", c=512)
                        for mi in range(4):
                            m = 4 * j + mi
                            for (t0, tn) in tiles:
                                pi, pb = ps_next()
                                mm(pb[:, 0:tn], [(w3[:, k, mi * 128:(mi + 1) * 128], ub[:, k, t0:t0 + tn]) for k in range(12)], wkeys + [('ub', k) for k in range(12)], ('ps', pi))
                                S.op('act', lambda e, pb=pb, m=m, t0=t0, tn=tn: e.activation(out=rsd[:, t0:t0 + tn], in_=pb[:, 0:tn], func=AF.Sigmoid, bias=bg[:, j2, m:m + 1], scale=1.0),
                                     reads=[('ps', pi), 'bg'], writes=['rsd'])
                                S.op('dve', lambda e, m=m, t0=t0, tn=tn: e.tensor_tensor(out=mix[:, m, t0:t0 + tn], in0=u[:, m, t0:t0 + tn], in1=rsd[:, t0:t0 + tn], op=ALU.mult),
                                     reads=[('u', m), 'rsd'], writes=['h'])
                    chk('c11')
                    out_proj(w_ssout, j2, mix, tiles)

                chk('f%d' % l)
                phase()
                rmsnorm_to_h(64 + l * 16, tiles)
                y = ar.take([128, FC, TT], BF16)
                abuf = [ar.take([128, 2 + TT]) for _ in range(2)]
                cbuf = [ar.take([128, TT]) for _ in range(2)]
                if last:
                    pvT = ar.take([128, FC, 2, SS])
                    sct = [ar.take([SS, 2, 128]) for _ in range(2)]
                    cxt = [ar.take([18, 128]) for _ in range(2)]
                    S.dma('sp', lambda e: e.dma_start(out=o_conv_s0[l], in_=sconv[l][:, 1, :]))
                    for c in range(FC):
                        sk = sct[c % 2]
                        S.dma('sp', lambda e, c=c, sk=sk: e.dma_start(out=sk, in_=sconv[l][:, :, c * 128:(c + 1) * 128]), writes=[('sct', c % 2)])
                        pi, pb = ps_next()
                        for r_ in range(2):
                            S.op('pe', lambda e, r_=r_, pb=pb, sk=sk: e.matmul(pb[:, r_ * SS:(r_ + 1) * SS], lhsT=sk[:, r_, :], rhs=ident[0:SS, 0:SS], start=True, stop=True),
                                 reads=[('sct', c % 2), 'cs'], writes=[('ps', pi)])
                        S.op('act', lambda e, c=c, pb=pb: e.activation(out=pvT[:, c, :, :], in_=pb[:, 0:2 * SS].rearrange("p (r b) -> p r b", b=SS), func=AF.Copy), reads=[('ps', pi)], writes=['pvT'])
                for j in range(22):
                    wt, wkeys = wload(w_up[l, j], 16 * 512)
                    w3 = wt[:, :].rearrange("p (k c) -> p k c", c=512)
                    for ci in range(2 if j < 21 else 1):
                        c = 2 * j + ci
                        ab = abuf[c % 2]; cb = cbuf[c % 2]
                        cwc = cw[:, l, c * 4:(c + 1) * 4]
                        pia, pba = ps_next()
                        mm(pba[:, :], [(w3[:, k, ci * 128:(ci + 1) * 128], h[:, k, 0:TP]) for k in range(16)], wkeys + ['h'], ('ps', pia))
                        pig, pbg = ps_next()
                        mm(pbg[:, :], [(w3[:, k, 256 + ci * 128:256 + (ci + 1) * 128], h[:, k, 0:TP]) for k in range(16)], wkeys + ['h'], ('ps', pig))
                        S.op('act', lambda e, ab=ab, c=c: e.activation(out=ab[:, 0:2], in_=halo[:, l, c, :], func=AF.Copy), reads=['halo'], writes=[('ab', c % 2)])
                        S.op('act', lambda e, ab=ab, pba=pba: e.activation(out=ab[:, 2:2 + TP], in_=pba[:, :], func=AF.Copy), reads=[('ps', pia)], writes=[('ab', c % 2)])
                        S.op('act', lambda e, ab=ab, c=c: e.activation(out=halo[:, l, c, :], in_=ab[:, TP:TP + 2], func=AF.Copy), reads=[('ab', c % 2)], writes=['halo'])
                        S.op('dve', lambda e, ab=ab, cb=cb, cwc=cwc: e.tensor_scalar(out=cb[:, 0:TP], in0=ab[:, 2:2 + TP], scalar1=cwc[:, 2:3], scalar2=cwc[:, 3:4], op0=ALU.mult, op1=ALU.add),
                             reads=[('ab', c % 2), 'cw'], writes=[('cb', c % 2)])
                        S.op('dve', lambda e, ab=ab, cb=cb, cwc=cwc: e.scalar_tensor_tensor(out=cb[:, 0:TP], in0=ab[:, 1:1 + TP], scalar=cwc[:, 1:2], in1=cb[:, 0:TP], op0=ALU.mult, op1=ALU.add),
                             reads=[('ab', c % 2), 'cw', ('cb', c % 2)], writes=[('cb', c % 2)])
                        S.op('dve', lambda e, ab=ab, cb=cb, cwc=cwc: e.scalar_tensor_tensor(out=cb[:, 0:TP], in0=ab[:, 0:TP], scalar=cwc[:, 0:1], in1=cb[:, 0:TP], op0=ALU.mult, op1=ALU.add),
                             reads=[('ab', c % 2), 'cw', ('cb', c % 2)], writes=[('cb', c % 2)])
                        S.op('act', lambda e, cb=cb: e.activation(out=cb[:, 0:TP], in_=cb[:, 0:TP], func=AF.Silu), reads=[('cb', c % 2)], writes=[('cb', c % 2)])
                        S.op('dve', lambda e, cb=cb, pbg=pbg, c=c: e.tensor_tensor(out=y[:, c, 0:TP], in0=cb[:, 0:TP], in1=pbg[:, :], op=ALU.mult), reads=[('cb', c % 2), ('ps', pig)], writes=[('y', c)])
                        if last:
                            pia, pba = ps_next()
                            mm(pba[:, 0:SS], [(w3[:, k, ci * 128:(ci + 1) * 128], h[:, k, TP:TT]) for k in range(16)], wkeys + ['h'], ('ps', pia))
                            pig, pbg = ps_next()
                            mm(pbg[:, 0:SS], [(w3[:, k, 256 + ci * 128:256 + (ci + 1) * 128], h[:, k, TP:TT]) for k in range(16)], wkeys + ['h'], ('ps', pig))
                            S.op('act', lambda e, ab=ab, pba=pba: e.activation(out=ab[:, 2 + TP:2 + TT], in_=pba[:, 0:SS], func=AF.Copy), reads=[('ps', pia)], writes=[('ab', c % 2)])
                            S.op('dve', lambda e, ab=ab, cb=cb, cwc=cwc: e.tensor_scalar(out=cb[:, TP:TT], in0=ab[:, 2 + TP:2 + TT], scalar1=cwc[:, 2:3], scalar2=cwc[:, 3:4], op0=ALU.mult, op1=ALU.add),
                                 reads=[('ab', c % 2), 'cw'], writes=[('cb', c % 2)])
                            S.op('dve', lambda e, cb=cb, cwc=cwc, c=c: e.scalar_tensor_tensor(out=cb[:, TP:TT], in0=pvT[:, c, 1, :], scalar=cwc[:, 1:2], in1=cb[:, TP:TT], op0=ALU.mult, op1=ALU.add),
                                 reads=['pvT', 'cw', ('cb', c % 2)], writes=[('cb', c % 2)])
                            S.op('dve', lambda e, cb=cb, cwc=cwc, c=c: e.scalar_tensor_tensor(out=cb[:, TP:TT], in0=pvT[:, c, 0, :], scalar=cwc[:, 0:1], in1=cb[:, TP:TT], op0=ALU.mult, op1=ALU.add),
                                 reads=['pvT', 'cw', ('cb', c % 2)], writes=[('cb', c % 2)])
                            S.op('act', lambda e, cb=cb: e.activation(out=cb[:, TP:TT], in_=cb[:, TP:TT], func=AF.Silu), reads=[('cb', c % 2)], writes=[('cb', c % 2)])
                            S.op('dve', lambda e, cb=cb, pbg=pbg, c=c: e.tensor_tensor(out=y[:, c, TP:TT], in0=cb[:, TP:TT], in1=pbg[:, 0:SS], op=ALU.mult), reads=[('cb', c % 2), ('ps', pig)], writes=[('y', c)])
                            pit, pbt = ps_next()
                            mm(pbt[0:18, 0:128], [(ab[:, TP:TP + 18], ident)], [('ab', c % 2), 'cs'], ('ps', pit))
                            cx = cxt[c % 2]
                            S.op('act', lambda e, pbt=pbt, cx=cx: e.activation(out=cx, in_=pbt[0:18, 0:128], func=AF.Copy), reads=[('ps', pit)], writes=[('cxt', c % 2)])
                            S.dma('sp', lambda e, cx=cx, c=c: e.dma_start(out=o_convx[l][:, c * 128:(c + 1) * 128], in_=cx), reads=[('cxt', c % 2)])
                ykeys = [('y', c) for c in range(FC)]
                for m in range(16):
                    wt, wkeys = wload(w_dn[l, m], FC * 128)
                    w3 = wt[:, 0:FC * 128].rearrange("p (k c) -> p k c", c=128)
                    for (t0, tn) in tiles:
                        pi, pb = ps_next()
                        mm(pb[:, 0:tn], [(w3[:, k, :], y[:, k, t0:t0 + tn]) for k in range(FC)], wkeys + ykeys, ('ps', pi))
                        S.op('dve', lambda e, pb=pb, m=m, t0=t0, tn=tn: e.tensor_tensor(out=x[:, m, t0:t0 + tn], in0=x[:, m, t0:t0 + tn], in1=pb[:, 0:tn], op=ALU.add),
                             reads=[('ps', pi), 'x'], writes=['x'])

            chk('ffn3')
            phase()
            yo = ar.take([128, 16, TT])
            rmsnorm_to_h(192, tiles, out_fp32=yo)
            ot = [ar.take([128, D]) for _ in range(2)]
            for tb in range(nblk):
                rows = 128 if tb < 4 else SS
                buf = ot[tb % 2]
                for cg in range(4):
                    pi, pb = ps_next()
                    for ci in range(4):
                        c = cg * 4 + ci
                        S.op('pe', lambda e, c=c, ci=ci, tb=tb, rows=rows, pb=pb: e.matmul(pb[0:rows, ci * 128:(ci + 1) * 128], lhsT=yo[:, c, tb * 128:tb * 128 + rows], rhs=ident, start=True, stop=True),
                             reads=['yout', 'cs'], writes=[('ps', pi)])
                    S.op('act', lambda e, pb=pb, buf=buf, cg=cg, rows=rows: e.activation(out=buf[0:rows, cg * 512:(cg + 1) * 512], in_=pb[0:rows, :], func=AF.Copy), reads=[('ps', pi)], writes=[('ot', tb % 2)])
                dst = o_y[p * TP + tb * 128:p * TP + tb * 128 + 128, :] if tb < 4 else o_ys
                S.dma('sp', lambda e, buf=buf, dst=dst, rows=rows: e.dma_start(out=dst, in_=buf[0:rows, :]), reads=[('ot', tb % 2)])

        print('arena peak', ar.peak, 'of', ARENA)
        print('instructions recorded:', S.nins, {e: len(S.prog[e]) for e in S.prog})
        S.emit(block)
    return nc


def _blk(W, bw):
    K, N = W.shape
    kc = K // 128
    nb = N // bw
    a = W.reshape(kc, 128, nb, bw).transpose(2, 1, 0, 3)
    return np.ascontiguousarray(a).reshape(nb, 128, kc * bw)


def _fm(v):
    return np.ascontiguousarray(v.reshape(-1, 128).T)


_CACHE = {}


def prep_small(g_mix, g_ffn, g_mem, g_final, sg_g_v, sg_w_s, sg_b_s, ssm_lam_re, ssm_lam_im, ssm_log_dt, ssm_b_re, ssm_b_im,
               ssm_c_re, ssm_c_im, ssm_d, ssm_b_glu, ffn_conv_w, ffn_conv_b):
    f = np.float32
    A = lambda a: np.asarray(a, dtype=f)
    sh = {}
    gvec = np.zeros((128, 13 * 16), f)
    for l in range(4):
        gvec[:, l * 16:(l + 1) * 16] = _fm(A(g_mix)[l])
        gvec[:, 64 + l * 16:64 + (l + 1) * 16] = _fm(A(g_ffn)[l])
        gvec[:, 128 + l * 16:128 + (l + 1) * 16] = _fm(A(g_mem)[l])
    gvec[:, 192:208] = _fm(A(g_final))
    sh['gvec'] = gvec
    sh['gv_bc'] = np.ascontiguousarray(np.broadcast_to(A(sg_g_v)[:, None, :], (2, 128, TW)))
    sh['wsT'] = np.ascontiguousarray(A(sg_w_s).transpose(0, 3, 1, 2)).reshape(2, 128, 12 * 128)
    sh['bs_bc'] = np.ascontiguousarray(np.broadcast_to(A(sg_b_s).reshape(2, 1, 12 * 128), (2, 128, 12 * 128)))
    sh['w00'] = np.ascontiguousarray(np.broadcast_to(A(sg_w_s)[:, None, :, 0, 0], (2, 16, 12)))
    lr = A(ssm_lam_re).transpose(0, 2, 1); li = A(ssm_lam_im).transpose(0, 2, 1)
    lamx = np.zeros((2, 128, 3, NG), f)
    lamx[:, :, 0, :] = np.concatenate([lr, lr], axis=1)
    lamx[:, :, 1, :] = np.concatenate([li, li], axis=1)
    lamx[:, :, 2, :] = A(ssm_log_dt)[:, None, :]
    sh['lam'] = lamx.reshape(2, 128, 3 * NG)
    bre = A(ssm_b_re).transpose(0, 2, 1, 3); bim = A(ssm_b_im).transpose(0, 2, 1, 3)
    sh['Bcat'] = np.ascontiguousarray(np.concatenate([bre, bim], axis=1)).reshape(2, 128, NG * 16)
    sh['Bsw'] = np.ascontiguousarray(np.concatenate([bim, bre], axis=1)).reshape(2, 128, NG * 16)
    cre = A(ssm_c_re).transpose(0, 3, 1, 2); cim = A(ssm_c_im).transpose(0, 3, 1, 2)
    sh['Ccat'] = np.ascontiguousarray(np.concatenate([cre, cim], axis=1)).reshape(2, 128, NG * 16)
    sh['Csw'] = np.ascontiguousarray(np.concatenate([cim, cre], axis=1)).reshape(2, 128, NG * 16)
    sh['dvec'] = np.stack([_fm(A(ssm_d)[l]) for l in range(2)])
    sh['bglu'] = np.stack([_fm(A(ssm_b_glu)[l]) for l in range(2)])
    cwv = np.zeros((4, 128, FC, 4), f)
    for l in range(4):
        for j in range(3):
            cwv[l, :, :, j] = _fm(A(ffn_conv_w)[l, j])
        cwv[l, :, :, 3] = _fm(A(ffn_conv_b)[l])
    sh['convw'] = cwv.reshape(4, 128, FC * 4)
    sh['cst'] = make_consts()
    return sh


def prep_core(c, b, x_prompt, x_sample, mem_prompt, cache_mem_k, cache_mem_v, state_ssm_re, state_ssm_im, state_conv):
    m = {}
    m['xp'] = np.ascontiguousarray(x_prompt[b])
    m['xs'] = np.ascontiguousarray(x_sample[c * SS:(c + 1) * SS, 0, :])
    m['mem'] = np.ascontiguousarray(mem_prompt[b])
    m['ck'] = np.ascontiguousarray(cache_mem_k[:, c * SS:(c + 1) * SS].reshape(4, SS, NMEM, 512))
    m['cv'] = np.ascontiguousarray(cache_mem_v[:, c * SS:(c + 1) * SS].reshape(4, SS, NMEM, 512))
    sre = state_ssm_re[:, c * SS:(c + 1) * SS]; sim = state_ssm_im[:, c * SS:(c + 1) * SS]
    m['ss_cat'] = np.ascontiguousarray(np.concatenate([sre, sim], axis=-1)).reshape(2, SS, NG * 128)
    m['ss_sw'] = np.ascontiguousarray(np.concatenate([sim, sre], axis=-1)).reshape(2, SS, NG * 128)
    m['sconv'] = np.ascontiguousarray(state_conv[:, c * SS:(c + 1) * SS])
    return m


def kernel(x_prompt, x_sample, mem_prompt, cache_mem_k, cache_mem_v, state_ssm_re, state_ssm_im,
           state_conv, g_mix, g_ffn, g_mem, g_final, w_mem_kv, sg_w_in, sg_w_out, sg_g_v, sg_w_s,
           sg_b_s, ssm_w_in, ssm_w_out, ssm_lam_re, ssm_lam_im, ssm_log_dt, ssm_b_re, ssm_b_im,
           ssm_c_re, ssm_c_im, ssm_d, ssm_w_glu, ssm_b_glu, ffn_w_up, ffn_conv_w, ffn_conv_b,
           ffn_w_down):
    f = np.float32
    A = lambda a: np.asarray(a, dtype=f)
    x_prompt, x_sample, mem_prompt = A(x_prompt), A(x_sample), A(mem_prompt)
    cache_mem_k, cache_mem_v = A(cache_mem_k), A(cache_mem_v)
    state_ssm_re, state_ssm_im, state_conv = A(state_ssm_re), A(state_ssm_im), A(state_conv)
    sh = prep_small(g_mix, g_ffn, g_mem, g_final, sg_g_v, sg_w_s, sg_b_s, ssm_lam_re, ssm_lam_im, ssm_log_dt, ssm_b_re, ssm_b_im,
                    ssm_c_re, ssm_c_im, ssm_d, ssm_b_glu, ffn_conv_w, ffn_conv_b)
    sh['w_kv'] = np.stack([_blk(A(w_mem_kv)[l], 512) for l in range(4)])
    sh['w_sgin'] = np.stack([_blk(A(sg_w_in)[l], 512) for l in range(2)])
    sh['w_sgout'] = np.stack([_blk(A(sg_w_out)[l], 512) for l in range(2)])
    sh['w_ssin'] = np.stack([_blk(A(ssm_w_in)[l], 512) for l in range(2)])
    sh['w_ssout'] = np.stack([_blk(A(ssm_w_out)[l], 512) for l in range(2)])
    sh['w_glu'] = np.stack([_blk(A(ssm_w_glu)[l], 512) for l in range(2)])
    wup = np.zeros((4, 22, 128, 16, 512), f)
    for l in range(4):
        W = A(ffn_w_up)[l]
        Wa = np.zeros((D, 22 * 256), f); Wg = np.zeros((D, 22 * 256), f)
        Wa[:, :DFF] = W[:, :DFF]; Wg[:, :DFF] = W[:, DFF:]
        wup[l, :, :, :, 0:256] = _blk(Wa, 256).reshape(22, 128, 16, 256)
        wup[l, :, :, :, 256:512] = _blk(Wg, 256).reshape(22, 128, 16, 256)
    sh['w_up'] = wup.reshape(4, 22, 128, 16 * 512)
    sh['w_dn'] = np.stack([_blk(A(ffn_w_down)[l], 128) for l in range(4)])
    in_maps = []
    for c in range(8):
        m = dict(sh)
        m.update(prep_core(c, c % NB, x_prompt, x_sample, mem_prompt, cache_mem_k, cache_mem_v, state_ssm_re, state_ssm_im, state_conv))
        in_maps.append(m)
    if 'nc' not in _CACHE:
        _CACHE['nc'] = build_program()
    nc = _CACHE['nc']
    res = run_bass_kernel_spmd(nc, in_maps, core_ids=list(range(8)))
    R = res.results
    y_prompt = np.stack([R[b]['o_y'] for b in range(NB)]).astype(f)
    y_sample = np.concatenate([R[c]['o_ys'] for c in range(8)], axis=0).reshape(NS, 1, D).astype(f)
    mk = np.stack([R[b]['o_mk'] for b in range(NB)], axis=1).reshape(4, NB, NMEM, 4, 128).astype(f)
    mv = np.stack([R[b]['o_mv'] for b in range(NB)], axis=1).reshape(4, NB, NMEM, 4, 128).astype(f)
    sp = np.stack([R[b]['o_ssm_p'] for b in range(NB)], axis=1)
    ssm_re_p = np.ascontiguousarray(sp[:, :, 0:64, :].transpose(0, 1, 3, 2)).astype(f)
    ssm_im_p = np.ascontiguousarray(sp[:, :, 64:128, :].transpose(0, 1, 3, 2)).astype(f)
    conv_p = np.stack([R[b]['o_convx'][:, 0:2, :] for b in range(NB)], axis=1).astype(f)
    ss_ = np.concatenate([R[c]['o_ssm_s'] for c in range(8)], axis=1).reshape(2, NS, NG, 128)
    ssm_re_s = np.ascontiguousarray(ss_[..., 0:64]).astype(f)
    ssm_im_s = np.ascontiguousarray(ss_[..., 64:128]).astype(f)
    c0 = np.concatenate([R[c]['o_conv_s0'] for c in range(8)], axis=1)
    c1 = np.concatenate([R[c]['o_convx'][:, 2:18, :] for c in range(8)], axis=1)
    conv_s = np.stack([c0, c1], axis=2).astype(f)
    sgv = np.concatenate([R[c]['o_sgv'] for c in range(8)], axis=1).reshape(2, NS, 1, TW).astype(f)
    return (y_prompt, y_sample, mk, mv, ssm_re_p, ssm_im_p, conv_p, ssm_re_s, ssm_im_s, conv_s, sgv)
```

```python
import contextlib
import os
import types
import numpy as np
import concourse.bass as bass
import concourse.mybir as mybir
from concourse.bass_utils import run_bass_kernel_spmd

F32 = mybir.dt.float32
BF16 = mybir.dt.bfloat16
I32 = mybir.dt.int32
AF = mybir.ActivationFunctionType
ALU = mybir.AluOpType
AX = mybir.AxisListType

D = 2048; DEPTH = 4; SEQ = 2048; NB = 4; NS = 128; NMEM = 256
TW = 1536; DFF = 5504; FC = 43; NG = 96
TP = 512; SS = 16; TT = TP + SS
NPASS = SEQ // TP
EPS = 1e-6
TWO_PI = float(2 * np.pi)


def _freeze(fn):
    if fn is None or fn.__closure__ is None:
        return fn
    cells = []
    for c in fn.__closure__:
        try:
            cells.append(types.CellType(c.cell_contents))
        except ValueError:
            cells.append(c)
    return types.FunctionType(fn.__code__, fn.__globals__, fn.__name__, fn.__defaults__, tuple(cells))


class Sched:
    ENG = ('pe', 'act', 'dve', 'pool', 'sp')

    def __init__(self, nc, stack, n_dma_sems=20):
        self.nc = nc
        self.prog = {e: [] for e in self.ENG}
        self.sem = {e: stack.enter_context(nc.semaphore('S_' + e)) for e in ('pe', 'act', 'dve', 'pool')}
        self.cnt = {e: 0 for e in self.sem}
        self.dsem = {q: [stack.enter_context(nc.semaphore('D_%s%d' % (q, i))) for i in range(n_dma_sems)]
                     for q in ('sp', 'pool')}
        self.dcnt = {q: [0] * n_dma_sems for q in ('sp', 'pool')}
        self.dnext = {q: 0 for q in ('sp', 'pool')}
        self.waited = {e: {} for e in self.ENG}
        self.res = {}
        self.nins = 0
        self.stopped = False

    def _need(self, eng, tok, waits):
        if tok is None:
            return
        sem, val = tok
        w = self.waited[eng]
        if w.get(sem.name, 0) >= val:
            return
        w[sem.name] = val
        waits.append((sem, val))

    def _deps(self, eng, reads, writes, waits):
        for k in reads:
            r = self.res.get(k)
            if r:
                self._need(eng, r['w'], waits)
                if isinstance(k, tuple) and k[0] == 'ps':
                    for sname, t in r['r'].items():
                        if sname != self.sem.get(eng, None) and (eng not in self.sem or sname != self.sem[eng].name):
                            self._need(eng, t, waits)
        for k in writes:
            r = self.res.get(k)
            if r:
                self._need(eng, r['w'], waits)
                for t in r['r'].values():
                    self._need(eng, t, waits)

    def _commit(self, tok, reads, writes):
        for k in reads:
            r = self.res.setdefault(k, {'w': None, 'r': {}})
            r['r'][tok[0].name] = tok
        for k in writes:
            self.res[k] = {'w': tok, 'r': {}}

    def op(self, eng, fns, reads=(), writes=()):
        if self.stopped:
            return None
        if not isinstance(fns, (list, tuple)):
            fns = [fns]
        fns = [_freeze(f) for f in fns]
        waits = []
        self._deps(eng, reads, writes, waits)
        self.cnt[eng] += 1
        tok = (self.sem[eng], self.cnt[eng])
        n = len(fns)
        for i, fn in enumerate(fns):
            self.prog[eng].append((fn, waits if i == 0 else [], (self.sem[eng], 1) if i == n - 1 else None))
        self.nins += n
        self._commit(tok, reads, writes)
        return tok

    def dma(self, q, fn, reads=(), writes=()):
        if self.stopped:
            return None
        waits = []
        i = self.dnext[q]
        self.dnext[q] = (i + 1) % len(self.dsem[q])
        sem = self.dsem[q][i]
        if self.dcnt[q][i] > 0:
            self._need(q, (sem, self.dcnt[q][i]), waits)
        self._deps(q, reads, writes, waits)
        self.dcnt[q][i] += 16
        tok = (sem, self.dcnt[q][i])
        self.prog[q].append((_freeze(fn), waits, (sem, 16)))
        self.nins += 1
        self._commit(tok, reads, writes)
        return tok

    def all_tokens(self):
        toks = [(self.sem[e], self.cnt[e]) for e in self.sem if self.cnt[e] > 0]
        for q in self.dsem:
            for i, s in enumerate(self.dsem[q]):
                if self.dcnt[q][i] > 0:
                    toks.append((s, self.dcnt[q][i]))
        return toks

    def barrier(self, keep=lambda k: False):
        if self.stopped:
            return
        toks = [(self.sem[e], self.cnt[e]) for e in ('pe', 'act', 'dve') if self.cnt[e] > 0]
        for q in ('sp', 'pool'):
            for i, s in enumerate(self.dsem[q]):
                if self.dcnt[q][i] > 0:
                    toks.append((s, self.dcnt[q][i]))
        for e in ('pe', 'act', 'dve', 'sp', 'pool'):
            waits = []
            for t in toks:
                self._need(e, t, waits)
            if waits:
                self.prog[e].append((None, waits, None))
        self.res = {k: v for k, v in self.res.items() if keep(k)}

    def emit(self, block):
        final = self.all_tokens()

        def run(e, engine):
            for fn, waits, inc in self.prog[e]:
                for sem, val in waits:
                    engine.wait_ge(sem, val)
                if fn is None:
                    continue
                ins = fn(engine)
                if inc is not None:
                    ins.then_inc(inc[0], inc[1])
            if e == 'sp':
                for sem, val in final:
                    engine.wait_ge(sem, val)

        @block.tensor
        def _(eng):
            run('pe', eng)

        @block.scalar
        def _(eng):
            run('act', eng)

        @block.vector
        def _(eng):
            run('dve', eng)

        @block.gpsimd
        def _(eng):
            run('pool', eng)

        @block.sync
        def _(eng):
            run('sp', eng)


C_ID = 0
C_JT = 128
C_TRI = 256
C_IOTA = 384
C_SGNC = 896
C_SGNB = 897
C_GMASK = 898
NCST = 906


def make_consts():
    c = np.zeros((128, NCST), np.float32)
    c[:, C_ID:C_ID + 128] = np.eye(128)
    J = np.zeros((128, 128), np.float32)
    for m in range(64):
        J[m, m + 64] = -1.0
        J[m + 64, m] = 1.0
    c[:, C_JT:C_JT + 128] = J.T
    s = np.arange(128)[:, None]; t = np.arange(128)[None, :]
    c[:, C_TRI:C_TRI + 128] = (t >= s).astype(np.float32)
    c[:, C_IOTA:C_IOTA + 512] = np.arange(512, dtype=np.float32)[None, :]
    c[:64, C_SGNC] = 1.0; c[64:, C_SGNC] = -1.0
    c[:64, C_SGNB] = -1.0; c[64:, C_SGNB] = 1.0
    for i in range(8):
        c[i * 16:(i + 1) * 16, C_GMASK + i] = 1.0
    return c


class _Stop(Exception):
    pass


def build_program():
    nc = bass.Bass("TRN2", target_bir_lowering=False)
    SMALLW = bool(os.environ.get('MK_SMALLW'))
    STOP = os.environ.get('MK_STOP', '')
    NPASS_ = int(os.environ.get('MK_NPASS', NPASS))

    def chk(name):
        if STOP == name and not S.stopped:
            S.dma('sp', lambda e: e.dma_start(out=o_dbgx, in_=x_[0][:].rearrange("p a b -> p (a b)")), reads=['x'])
            S.dma('pool', lambda e: e.dma_start(out=o_dbgh, in_=h_[0][:].rearrange("p a b -> p (a b)")), reads=['h'])
            if dbg_u[0] is not None:
                S.dma('sp', lambda e: e.dma_start(out=o_dbgu, in_=dbg_u[0].rearrange("p a b -> p (a b)")), reads=['u'] + [('u', m) for m in range(12)])
            S.stopped = True
    x_ = [None]; h_ = [None]

    class WSel:
        def __init__(self, ap):
            self.ap = ap

        def __getitem__(self, idx):
            if not isinstance(idx, tuple):
                idx = (idx,)
            if SMALLW:
                idx = tuple(0 for _ in idx)
            return self.ap[idx]

    def din(name, shape):
        return nc.dram_tensor(name, list(shape), F32, kind="ExternalInput").ap()

    def dout(name, shape):
        return nc.dram_tensor(name, list(shape), F32, kind="ExternalOutput").ap()

    xp = din('xp', [SEQ, D]); xs = din('xs', [SS, D]); mem = din('mem', [NMEM, D])
    ck = din('ck', [DEPTH, SS, NMEM, 512]); cv = din('cv', [DEPTH, SS, NMEM, 512])
    ss_cat = din('ss_cat', [2, SS, NG * 128]); ss_sw = din('ss_sw', [2, SS, NG * 128])
    sconv = din('sconv', [DEPTH, SS, 2, DFF])
    gvec = din('gvec', [128, 13 * 16])
    gv_bc = din('gv_bc', [2, 128, TW]); wsT = din('wsT', [2, 128, 12 * 128]); bs_bc = din('bs_bc', [2, 128, 12 * 128])
    w00 = din('w00', [2, 16, 12])
    lam = din('lam', [2, 128, 3 * NG])
    Bcat = din('Bcat', [2, 128, NG * 16]); Bsw = din('Bsw', [2, 128, NG * 16])
    Ccat = din('Ccat', [2, 128, NG * 16]); Csw = din('Csw', [2, 128, NG * 16])
    dvec = din('dvec', [2, 128, 12]); bglu = din('bglu', [2, 128, 12])
    convw = din('convw', [DEPTH, 128, FC * 4])
    cst = din('cst', [128, NCST])
    w_kv = WSel(din('w_kv', ([1, 1] + [DEPTH, 2, 128, 16 * 512][2:]) if SMALLW else [DEPTH, 2, 128, 16 * 512]))
    w_sgin = WSel(din('w_sgin', ([1, 1] + [2, 7, 128, 16 * 512][2:]) if SMALLW else [2, 7, 128, 16 * 512])); w_sgout = WSel(din('w_sgout', ([1, 1] + [2, 4, 128, 16 * 512][2:]) if SMALLW else [2, 4, 128, 16 * 512]))
    w_ssin = WSel(din('w_ssin', ([1, 1] + [2, 4, 128, 16 * 512][2:]) if SMALLW else [2, 4, 128, 16 * 512])); w_ssout = WSel(din('w_ssout', ([1, 1] + [2, 4, 128, 16 * 512][2:]) if SMALLW else [2, 4, 128, 16 * 512]))
    w_glu = WSel(din('w_glu', ([1, 1] + [2, 3, 128, 12 * 512][2:]) if SMALLW else [2, 3, 128, 12 * 512]))
    w_up = WSel(din('w_up', ([1, 1] + [DEPTH, 22, 128, 16 * 512][2:]) if SMALLW else [DEPTH, 22, 128, 16 * 512])); w_dn = WSel(din('w_dn', ([1, 1] + [DEPTH, 16, 128, FC * 128][2:]) if SMALLW else [DEPTH, 16, 128, FC * 128]))

    o_y = dout('o_y', [SEQ, D]); o_ys = dout('o_ys', [SS, D])
    o_mk = dout('o_mk', [DEPTH, NMEM, 512]); o_mv = dout('o_mv', [DEPTH, NMEM, 512])
    o_ssm_p = dout('o_ssm_p', [2, 128, NG]); o_convx = dout('o_convx', [DEPTH, 18, DFF])
    o_conv_s0 = dout('o_conv_s0', [DEPTH, SS, DFF]); o_ssm_s = dout('o_ssm_s', [2, SS, NG * 128])
    o_sgv = dout('o_sgv', [2, SS, TW])
    DBG = bool(os.environ.get('MK_STOP'))
    if DBG:
        o_dbgx = dout('o_dbgx', [128, 16 * TT]); o_dbgh = dout('o_dbgh', [128, 16 * TT]); o_dbgu = dout('o_dbgu', [128, 12 * TT])
    dbg_u = [None]

    with contextlib.ExitStack() as st:
        def sb(name, shape, dt=F32):
            return st.enter_context(nc.sbuf_tensor(name, list(shape), dt))

        x = sb('x', [128, 16, TT])
        h = sb('h', [128, 16, TT], BF16)
        x_[0] = x; h_[0] = h
        cs = sb('cs', [128, NCST])
        identb = sb('identb', [128, 128], BF16); onesb = sb('onesb', [128, 128], BF16)
        gv = sb('gv', [128, 13 * 16])
        epsb = sb('epsb', [128, 1]); hpib = sb('hpib', [128, 1])
        KT = sb('KT', [128, DEPTH, 4, NMEM], BF16); Vt = sb('Vt', [128, DEPTH, 2, 512], BF16)
        wr = [sb('wr%d' % i, [128, 16 * 512], BF16) for i in range(2)]
        halo = sb('halo', [128, DEPTH, FC, 2])
        cw = sb('cw', [128, DEPTH, FC * 4])
        th = sb('th', [128, 2, NG]); rho = sb('rho', [128, 2, NG])
        KR = sb('KR', [128, 2, NG]); KI = sb('KI', [128, 2, NG]); KRs = sb('KRs', [128, 2, NG]); KIs = sb('KIs', [128, 2, NG])
        c512 = sb('c512', [128, 2, NG]); s512 = sb('s512', [128, 2, NG]); c511 = sb('c511', [128, 2, NG]); s511 = sb('s511', [128, 2, NG])
        A1 = sb('A1', [128, 2, NG]); A2s = sb('A2s', [128, 2, NG])
        winit = sb('winit', [128, 2, NG])
        dv = sb('dv', [128, 2, 12]); bg = sb('bg', [128, 2, 12])
        rsd = sb('rsd', [128, TT]); rstd = sb('rstd', [128, TT])
        ARENA = 87 * 1024
        arena = sb('arena', [128, ARENA // 2], BF16)
        pbank = [st.enter_context(nc.psum_tensor('pb%d' % i, [128, 512], F32)) for i in range(8)]

        S = Sched(nc, st)
        block = st.enter_context(nc.Block())

        class Ar:
            def __init__(self):
                self.off = 0

            def reset(self):
                self.off = 0

            def take(self, shape, dt=F32):
                esz = 4 if dt in (F32, I32) else 2
                n = int(np.prod(shape[1:]))
                nbytes = ((n * esz + 63) // 64) * 64
                assert self.off + nbytes <= ARENA, (self.off, nbytes, shape)
                a = arena[0:shape[0], self.off // 2:(self.off + n * esz) // 2]
                if dt != BF16:
                    a = a.bitcast(dt)
                self.off += nbytes
                self.peak = max(getattr(self, 'peak', 0), self.off)
                if len(shape) == 3:
                    a = a.rearrange("p (a b) -> p a b", b=shape[2])
                elif len(shape) == 4:
                    a = a.rearrange("p (a b c) -> p a b c", b=shape[2], c=shape[3])
                return a
        ar = Ar()

        psn = [0]

        def ps_next():
            i = psn[0] % 6
            psn[0] += 1
            return i, pbank[i]

        wslot = [0]

        def wload(src2d, ncols, key_extra=None):
            i = wslot[0] % 2
            wslot[0] += 1
            key = ('w', i)
            t = wr[i]
            half = ncols // 2
            if not os.environ.get('MK_NOWDMA'):
                S.dma('pool', lambda e: e.dma_start(out=t[:, 0:half], in_=src2d[:, 0:half]), writes=[(key, 0)])
                S.dma('pool', lambda e: e.dma_start(out=t[:, half:ncols], in_=src2d[:, half:ncols]), writes=[(key, 1)])
            return t, [(key, 0), (key, 1)]

        def keepw(k):
            return isinstance(k, tuple) and len(k) == 2 and isinstance(k[0], tuple) and k[0][0] == 'w'

        def phase():
            S.barrier(keep=keepw)
            ar.reset()

        def mm(out_ap, pairs, reads, pskey):
            n = len(pairs)
            fns = []
            for i, (l, r) in enumerate(pairs):
                fns.append(lambda e, l=l, r=r, i=i: e.matmul(out_ap, lhsT=l, rhs=r, start=(i == 0), stop=(i == n - 1)))
            S.op('pe', fns, reads=reads, writes=[pskey])

        S.dma('sp', lambda e: e.dma_start(out=cs[:], in_=cst), writes=['cs'])
        S.dma('sp', lambda e: e.dma_start(out=gv[:], in_=gvec), writes=['gv'])
        for l in range(DEPTH):
            S.dma('sp', lambda e, l=l: e.dma_start(out=cw[:, l, :], in_=convw[l]), writes=['cw'])
        for l in range(2):
            S.dma('sp', lambda e, l=l: e.dma_start(out=dv[:, l, :], in_=dvec[l]), writes=['dv'])
            S.dma('sp', lambda e, l=l: e.dma_start(out=bg[:, l, :], in_=bglu[l]), writes=['bg'])
        S.op('dve', lambda e: e.tensor_copy(out=identb[:], in_=cs[:, C_ID:C_ID + 128]), reads=['cs'], writes=['identb'])
        S.op('dve', lambda e: e.memset(onesb[:], 1.0), writes=['onesb'])
        S.op('dve', lambda e: e.memset(epsb[:], EPS), writes=['epsb'])
        S.op('dve', lambda e: e.memset(hpib[:], float(np.pi / 2)), writes=['hpib'])
        S.op('dve', lambda e: e.memset(halo[:], 0.0), writes=['halo'])
        S.op('dve', lambda e: e.memset(winit[:], 0.0), writes=['winit'])
        ident = cs[:, C_ID:C_ID + 128]
        JT = cs[:, C_JT:C_JT + 128]
        tri = cs[:, C_TRI:C_TRI + 128]
        iota = cs[:, C_IOTA:C_IOTA + 512]

        def sincos(ang, n, out_sin, out_cos, tmpi, tmpf, tmpr, key_in, key_s, key_c, tag):
            S.op('dve', lambda e: e.tensor_scalar(out=tmpi, in0=ang, scalar1=float(1 / TWO_PI), scalar2=None, op0=ALU.mult),
                 reads=[key_in], writes=[tag + 'i'])
            S.op('dve', lambda e: e.tensor_copy(out=tmpf, in_=tmpi), reads=[tag + 'i'], writes=[tag + 'f'])
            S.op('dve', lambda e: e.scalar_tensor_tensor(out=tmpr, in0=tmpf, scalar=-TWO_PI, in1=ang, op0=ALU.mult, op1=ALU.add),
                 reads=[tag + 'f', key_in], writes=[tag + 'r'])
            S.op('act', lambda e: e.activation(out=out_sin, in_=tmpr, func=AF.Sin), reads=[tag + 'r'], writes=[key_s])
            S.op('dve', lambda e: e.scalar_tensor_tensor(out=tmpf, in0=tmpr, scalar=-1.0, in1=tmpr, op0=ALU.mult, op1=ALU.max), reads=[tag + 'r'], writes=[tag + 'f'])
            S.op('act', lambda e: e.activation(out=out_cos, in_=tmpf, func=AF.Sin, bias=hpib[:, 0:1], scale=-1.0),
                 reads=[tag + 'f', 'hpib'], writes=[key_c])

        chk('c0')
        ar.reset()
        lm = ar.take([128, 3, NG]); dt_ = ar.take([128, NG]); ang = ar.take([128, NG])
        ti = ar.take([128, NG], I32); tf = ar.take([128, NG]); tr = ar.take([128, NG])
        sn = ar.take([128, NG]); cn = ar.take([128, NG]); nr = ar.take([128, NG]); ni = ar.take([128, NG])
        den = ar.take([128, NG]); t1 = ar.take([128, NG]); t2 = ar.take([128, NG])
        for l in range(2):
            S.dma('sp', lambda e, l=l: e.dma_start(out=lm.rearrange("p a b -> p (a b)"), in_=lam[l]), writes=['lm'])
            S.op('act', lambda e: e.activation(out=dt_, in_=lm[:, 2, :], func=AF.Exp), reads=['lm'], writes=['dt'])
            S.op('dve', lambda e: e.tensor_tensor(out=t1, in0=lm[:, 0, :], in1=dt_, op=ALU.mult), reads=['lm', 'dt'], writes=['t1'])
            S.op('act', lambda e, l=l: e.activation(out=rho[:, l, :], in_=t1, func=AF.Exp), reads=['t1'], writes=['rho'])
            S.op('dve', lambda e, l=l: e.tensor_tensor(out=th[:, l, :], in0=lm[:, 1, :], in1=dt_, op=ALU.mult), reads=['lm', 'dt'], writes=['th'])
            sincos(th[:, l, :], NG, sn, cn, ti, tf, tr, 'th', 'sn', 'cn', 'p')
            S.op('dve', lambda e, l=l: e.tensor_tensor(out=nr, in0=rho[:, l, :], in1=cn, op=ALU.mult), reads=['rho', 'cn'], writes=['nr'])
            S.op('dve', lambda e: e.tensor_scalar(out=nr, in0=nr, scalar1=-1.0, scalar2=None, op0=ALU.add), reads=['nr'], writes=['nr'])
            S.op('dve', lambda e, l=l: e.tensor_tensor(out=ni, in0=rho[:, l, :], in1=sn, op=ALU.mult), reads=['rho', 'sn'], writes=['ni'])
            S.op('dve', lambda e, l=l: e.tensor_tensor(out=A1[:, l, :], in0=rho[:, l, :], in1=cn, op=ALU.mult), reads=['rho', 'cn'], writes=['A1'])
            S.op('dve', lambda e, l=l: e.tensor_scalar(out=A2s[:, l, :], in0=ni, scalar1=cs[:, C_SGNB:C_SGNB + 1], scalar2=None, op0=ALU.mult),
                 reads=['ni', 'cs'], writes=['A2s'])
            S.op('dve', lambda e: e.tensor_tensor(out=den, in0=lm[:, 0, :], in1=lm[:, 0, :], op=ALU.mult), reads=['lm'], writes=['den'])
            S.op('dve', lambda e: e.tensor_tensor(out=t1, in0=lm[:, 1, :], in1=lm[:, 1, :], op=ALU.mult), reads=['lm'], writes=['t1'])
            S.op('dve', lambda e: e.tensor_tensor(out=den, in0=den, in1=t1, op=ALU.add), reads=['den', 't1'], writes=['den'])
            S.op('dve', lambda e: e.reciprocal(out=den, in_=den), reads=['den'], writes=['den'])
            S.op('dve', lambda e: e.tensor_tensor(out=t1, in0=nr, in1=lm[:, 0, :], op=ALU.mult), reads=['nr', 'lm'], writes=['t1'])
            S.op('dve', lambda e: e.tensor_tensor(out=t2, in0=ni, in1=lm[:, 1, :], op=ALU.mult), reads=['ni', 'lm'], writes=['t2'])
            S.op('dve', lambda e: e.tensor_tensor(out=t1, in0=t1, in1=t2, op=ALU.add), reads=['t1', 't2'], writes=['t1'])
            S.op('dve', lambda e, l=l: e.tensor_tensor(out=KR[:, l, :], in0=t1, in1=den, op=ALU.mult), reads=['t1', 'den'], writes=['KR'])
            S.op('dve', lambda e: e.tensor_tensor(out=t1, in0=ni, in1=lm[:, 0, :], op=ALU.mult), reads=['ni', 'lm'], writes=['t1'])
            S.op('dve', lambda e: e.tensor_tensor(out=t2, in0=nr, in1=lm[:, 1, :], op=ALU.mult), reads=['nr', 'lm'], writes=['t2'])
            S.op('dve', lambda e: e.tensor_tensor(out=t1, in0=t1, in1=t2, op=ALU.subtract), reads=['t1', 't2'], writes=['t1'])
            S.op('dve', lambda e, l=l: e.tensor_tensor(out=KI[:, l, :], in0=t1, in1=den, op=ALU.mult), reads=['t1', 'den'], writes=['KI'])
            S.op('dve', lambda e, l=l: e.tensor_scalar(out=KIs[:, l, :], in0=KI[:, l, :], scalar1=cs[:, C_SGNB:C_SGNB + 1], scalar2=None, op0=ALU.mult),
                 reads=['KI', 'cs'], writes=['KIs'])
            S.op('dve', lambda e, l=l: e.tensor_scalar(out=KRs[:, l, :], in0=KR[:, l, :], scalar1=cs[:, C_SGNB:C_SGNB + 1], scalar2=None, op0=ALU.mult),
                 reads=['KR', 'cs'], writes=['KRs'])
            for (mult, cc, ssn, kc, ks) in ((512.0, c512, s512, 'c512', 's512'), (511.0, c511, s511, 'c511', 's511')):
                S.op('dve', lambda e, l=l, mult=mult: e.tensor_scalar(out=ang, in0=th[:, l, :], scalar1=mult, scalar2=None, op0=ALU.mult),
                     reads=['th'], writes=['ang'])
                sincos(ang, NG, ssn[:, l, :], cc[:, l, :], ti, tf, tr, 'ang', ks, kc, 'q')

        chk('c1')
        phase()
        memT = ar.take([128, 16, NMEM])
        mh = ar.take([128, 16, NMEM], BF16)
        mtok = [ar.take([128, D]) for _ in range(2)]
        mrs = ar.take([128, NMEM])
        kvo = [ar.take([128, 512]) for _ in range(2)]
        for tb in range(2):
            S.dma('sp', lambda e, tb=tb: e.dma_start(out=mtok[tb], in_=mem[tb * 128:(tb + 1) * 128, :]), writes=[('mtok', tb)])
        for c in range(16):
            pi, pb = ps_next()
            for tb in range(2):
                S.op('pe', lambda e, c=c, tb=tb, pb=pb: e.matmul(pb[:, tb * 128:(tb + 1) * 128], lhsT=mtok[tb][:, c * 128:(c + 1) * 128],
                                                                 rhs=ident, start=True, stop=True),
                     reads=[('mtok', tb), 'cs'], writes=[('ps', pi)])
            S.op('act', lambda e, c=c, pb=pb: e.activation(out=memT[:, c, :], in_=pb[:, 0:NMEM], func=AF.Copy), reads=[('ps', pi)], writes=['memT'])
        chk('k0')
        S.op('act', lambda e: e.activation(out=mh, in_=memT, func=AF.Square), reads=['memT'], writes=['mh'])
        pi, pb = ps_next()
        mm(pb[:, 0:NMEM], [(onesb[:], mh[:, c, :]) for c in range(16)], ['onesb', 'mh'], ('ps', pi))
        S.op('act', lambda e, pb=pb: e.activation(out=mrs, in_=pb[:, 0:NMEM], func=AF.Sqrt, bias=epsb[:, 0:1], scale=1.0 / D),
             reads=[('ps', pi), 'epsb'], writes=['mrs'])
        S.op('dve', lambda e: e.reciprocal(out=mrs, in_=mrs), reads=['mrs'], writes=['mrs'])
        chk('k1')
        for l in range(DEPTH):
            for c in range(16):
                S.op('dve', lambda e, l=l, c=c: e.scalar_tensor_tensor(out=mh[:, c, :], in0=memT[:, c, :], scalar=gv[:, 128 + l * 16 + c:128 + l * 16 + c + 1],
                                                                      in1=mrs, op0=ALU.mult, op1=ALU.mult),
                     reads=['memT', 'gv', 'mrs'], writes=['mh'])
            chk('k2_%d' % l)
            wk, kk = wload(w_kv[l, 0], 16 * 512)
            wv, kv_ = wload(w_kv[l, 1], 16 * 512)
            wk3 = wk[:, :].rearrange("p (k c) -> p k c", c=512)
            wv3 = wv[:, :].rearrange("p (k c) -> p k c", c=512)
            for hd in range(4):
                pi, pb = ps_next()
                mm(pb[:, 0:NMEM], [(wk3[:, k, hd * 128:(hd + 1) * 128], mh[:, k, :]) for k in range(16)], kk + ['mh'], ('ps', pi))
                S.op('act', lambda e, l=l, hd=hd, pb=pb: e.activation(out=KT[:, l, hd, :], in_=pb[:, 0:NMEM], func=AF.Copy),
                     reads=[('ps', pi)], writes=['KT'])
            chk('k3_%d' % l)
            for which, w3, wkeys, odst in ((0, wk3, kk, o_mk), (1, wv3, kv_, o_mv)):
                for mt in range(2):
                    pi, pb = ps_next()
                    _v = os.environ.get('MK_VAR')
                    if _v == 'c':
                        mm(pb[:, 0:256], [(mh[:, k, mt * 128:(mt + 1) * 128], w3[:, k, 0:256]) for k in range(16)], wkeys + ['mh'], ('ps', pi))
                        mm(pb[:, 256:512], [(mh[:, k, mt * 128:(mt + 1) * 128], w3[:, k, 256:512]) for k in range(16)], wkeys + ['mh'], ('ps', pi))
                    elif _v in ('d', 'e', 'f', 'g', 'i'):
                        pass
                    else:
                        mm(pb[:, :], [(mh[:, k, mt * 128:(mt + 1) * 128], w3[:, k, :]) for k in range(16)], wkeys + ['mh'], ('ps', pi))
                    buf = kvo[mt]
                    if os.environ.get('MK_VAR') not in ('f', 'g'):
                        S.op('act', lambda e, pb=pb, buf=buf: e.activation(out=buf, in_=pb[:, :], func=AF.Copy), reads=[('ps', pi)], writes=[('kvo', mt)])
                    if which == 1 and os.environ.get('MK_VAR') not in ('b', 'e', 'f', 'i'):
                        if os.environ.get('MK_VAR2') == 'j':
                            S.op('act', lambda e, pb=pb, l=l, mt=mt: e.activation(out=Vt[:, l, mt, :], in_=pb[:, :], func=AF.Copy), reads=[('ps', pi)], writes=['Vt'])
                        elif os.environ.get('MK_VAR2') == 'k':
                            S.op('dve', lambda e, buf=buf, l=l, mt=mt: e.tensor_copy(out=Vt[:, l, mt, :], in_=buf), reads=[('kvo', mt)], writes=['Vt'])
                        else:
                            S.op('dve', lambda e, pb=pb, l=l, mt=mt: e.tensor_copy(out=Vt[:, l, mt, :], in_=pb[:, :]), reads=[('ps', pi)], writes=['Vt'])
                    if os.environ.get('MK_VAR') not in ('a', 'e', 'f', 'g'):
                        S.dma('sp', lambda e, buf=buf, odst=odst, l=l, mt=mt: e.dma_start(out=odst[l, mt * 128:(mt + 1) * 128, :], in_=buf),
                              reads=[('kvo', mt)])
            chk('k4_%d' % l)

        chk('c2')
        def rmsnorm_to_h(goff, tiles, out_fp32=None):
            for (t0, tn) in tiles:
                S.op('act', lambda e, t0=t0, tn=tn: e.activation(out=h[:, :, t0:t0 + tn], in_=x[:, :, t0:t0 + tn], func=AF.Square),
                     reads=['x'], writes=['h'])
                pi, pb = ps_next()
                mm(pb[:, 0:tn], [(onesb[:], h[:, c, t0:t0 + tn]) for c in range(16)], ['onesb', 'h'], ('ps', pi))
                S.op('act', lambda e, pb=pb, t0=t0, tn=tn: e.activation(out=rsd[:, t0:t0 + tn], in_=pb[:, 0:tn], func=AF.Sqrt,
                                                                        bias=epsb[:, 0:1], scale=1.0 / D),
                     reads=[('ps', pi), 'epsb'], writes=['rsd'])
                S.op('dve', lambda e, t0=t0, tn=tn: e.reciprocal(out=rstd[:, t0:t0 + tn], in_=rsd[:, t0:t0 + tn]), reads=['rsd'], writes=['rstd'])
                for c in range(16):
                    dst = h if out_fp32 is None else out_fp32
                    S.op('dve', lambda e, c=c, t0=t0, tn=tn, dst=dst: e.scalar_tensor_tensor(
                        out=dst[:, c, t0:t0 + tn], in0=x[:, c, t0:t0 + tn], scalar=gv[:, goff + c:goff + c + 1],
                        in1=rstd[:, t0:t0 + tn], op0=ALU.mult, op1=ALU.mult),
                        reads=['x', 'gv', 'rstd'], writes=['h' if out_fp32 is None else 'yout'])

        def xattn_prompt(l, q, mix, pT, rz):
            sc = float(128 ** -0.5)
            for hd in range(4):
                for mt in range(2):
                    pi, pb = ps_next()
                    mm(pb[:, :], [(KT[:, l, hd, mt * 128:(mt + 1) * 128], q[:, hd, 0:TP])], ['KT', 'q'], ('ps', pi))
                    S.op('act', lambda e, pb=pb, mt=mt: e.activation(out=pT[:, mt, :], in_=pb[:, :], func=AF.Exp, scale=sc),
                         reads=[('ps', pi)], writes=[('pT', mt)])
                pi, pbz = ps_next()
                mm(pbz[:, :], [(onesb[:], pT[:, mt, :]) for mt in range(2)], ['onesb', ('pT', 0), ('pT', 1)], ('ps', pi))
                S.op('dve', lambda e, pbz=pbz: e.reciprocal(out=rz, in_=pbz[:, :]), reads=[('ps', pi)], writes=['rz'])
                pi, pbo = ps_next()
                mm(pbo[:, :], [(Vt[:, l, mt, hd * 128:(hd + 1) * 128], pT[:, mt, :]) for mt in range(2)], ['Vt', ('pT', 0), ('pT', 1)], ('ps', pi))
                S.op('dve', lambda e, pbo=pbo, hd=hd: e.tensor_tensor(out=mix[:, 12 + hd, 0:TP], in0=pbo[:, :], in1=rz, op=ALU.mult),
                     reads=[('ps', pi), 'rz'], writes=['h'])

        def xattn_sample(l, wq3, wqkeys, mix, abuf):
            sc = float(128 ** -0.5)
            qtok, qm, kb, vb, prod, scr, E, rzs = abuf['qtok'], abuf['qm'], abuf['kb'], abuf['vb'], abuf['prod'], abuf['scr'], abuf['E'], abuf['rzs']
            pi, pb = ps_next()
            mm(pb[0:SS, :], [(h[:, k, TP:TT], wq3[:, k, :]) for k in range(16)], wqkeys + ['h'], ('ps', pi))
            S.op('act', lambda e, pb=pb: e.activation(out=qtok, in_=pb[0:SS, :], func=AF.Copy), reads=[('ps', pi)], writes=['qtok'])
            for b in range(SS):
                S.dma('pool', lambda e, b=b: e.dma_start(out=kb, in_=ck[l, b].rearrange("(mt m) f -> m mt f", m=128)), writes=['kb'])
                S.op('dve', lambda e, b=b: e.tensor_scalar(out=qm, in0=qtok, scalar1=cs[0:SS, C_ID + b:C_ID + b + 1], scalar2=None, op0=ALU.mult),
                     reads=['qtok', 'cs'], writes=['qm'])
                pi, pbq = ps_next()
                mm(pbq[:, :], [(onesb[0:SS, :], qm)], ['onesb', 'qm'], ('ps', pi))
                for mt in range(2):
                    S.op('dve', lambda e, mt=mt, pbq=pbq: e.tensor_tensor(out=prod, in0=kb[:, mt, :], in1=pbq[:, :], op=ALU.mult),
                         reads=['kb', ('ps', pi)], writes=['prod'])
                    S.op('dve', lambda e, b=b, mt=mt: e.tensor_reduce(out=scr[:, b, mt, :], in_=prod.rearrange("p (h d) -> p h d", d=128),
                                                                      axis=AX.X, op=ALU.add),
                         reads=['prod'], writes=['scr'])
            S.op('act', lambda e: e.activation(out=E, in_=scr, func=AF.Exp, scale=sc), reads=['scr'], writes=['E'])
            pi, pbz = ps_next()
            mm(pbz[:, 0:128], [(onesb[:], E.rearrange("p b m h -> p (b m h)"))], ['onesb', 'E'], ('ps', pi))
            zv = pbz[:, 0:128].rearrange("p (b m h) -> p b m h", m=2, h=4)
            S.op('dve', lambda e, zv=zv: e.tensor_copy(out=rzs, in_=zv[:, :, 0, :]), reads=[('ps', pi)], writes=['rzs'])
            S.op('dve', lambda e, zv=zv: e.tensor_tensor(out=rzs, in0=rzs, in1=zv[:, :, 1, :], op=ALU.add), reads=[('ps', pi), 'rzs'], writes=['rzs'])
            S.op('dve', lambda e: e.reciprocal(out=rzs, in_=rzs), reads=['rzs'], writes=['rzs'])
            pbo = pbank[7]
            for b in range(SS):
                S.dma('pool', lambda e, b=b: e.dma_start(out=vb, in_=cv[l, b].rearrange("(mt m) f -> m mt f", m=128)), writes=['vb'])
                for hd in range(4):
                    mm(pbo[:, hd * SS + b:hd * SS + b + 1],
                       [(vb[:, mt, hd * 128:(hd + 1) * 128], E[:, b, mt, hd:hd + 1]) for mt in range(2)], ['vb', 'E'], ('ps', 7))
            for hd in range(4):
                S.op('dve', lambda e, hd=hd: e.tensor_tensor(out=mix[:, 12 + hd, TP:TT], in0=pbo[:, hd * SS:(hd + 1) * SS], in1=rzs[:, :, hd], op=ALU.mult),
                     reads=[('ps', 7), 'rzs'], writes=['h'])

        def xattn_bufs():
            return dict(qtok=ar.take([SS, 512], BF16), qm=ar.take([SS, 512], BF16), kb=ar.take([128, 2, 512], BF16),
                        vb=ar.take([128, 2, 512], BF16), prod=ar.take([128, 512]),
                        scr=ar.take([128, SS, 2, 4]), E=ar.take([128, SS, 2, 4], BF16), rzs=ar.take([128, SS, 4]))

        def subphase(off):
            S.barrier(keep=keepw)
            ar.off = off

        def out_proj(wsrc, j2, mix, tiles):
            for j in range(4):
                wt, wkeys = wload(wsrc[j2, j], 16 * 512)
                w3 = wt[:, :].rearrange("p (k c) -> p k c", c=512)
                for mi in range(4):
                    m = 4 * j + mi
                    for (t0, tn) in tiles:
                        pi, pb = ps_next()
                        mm(pb[:, 0:tn], [(w3[:, k, mi * 128:(mi + 1) * 128], mix[:, k, t0:t0 + tn]) for k in range(16)], wkeys + ['h'], ('ps', pi))
                        S.op('dve', lambda e, pb=pb, m=m, t0=t0, tn=tn: e.tensor_tensor(out=x[:, m, t0:t0 + tn], in0=x[:, m, t0:t0 + tn], in1=pb[:, 0:tn], op=ALU.add),
                             reads=[('ps', pi), 'x'], writes=['x'])

        for p in range(NPASS_):
            last = (p == NPASS_ - 1)
            tiles = [(0, TP)] + ([(TP, SS)] if last else [])
            phase()
            xt = [ar.take([128, D]) for _ in range(2)]
            nblk = 4 + (1 if last else 0)
            for tb in range(nblk):
                rows = 128 if tb < 4 else SS
                src = xp[p * TP + tb * 128:p * TP + tb * 128 + 128, :] if tb < 4 else xs
                buf = xt[tb % 2]
                S.dma('sp', lambda e, buf=buf, src=src, rows=rows: e.dma_start(out=buf[0:rows, :], in_=src), writes=[('xt', tb % 2)])
                for cg in range(4):
                    pi, pb = ps_next()
                    for ci in range(4):
                        c = cg * 4 + ci
                        S.op('pe', lambda e, buf=buf, rows=rows, c=c, ci=ci, pb=pb: e.matmul(
                            pb[:, ci * 128:ci * 128 + rows], lhsT=buf[0:rows, c * 128:(c + 1) * 128], rhs=ident[0:rows, 0:rows], start=True, stop=True),
                            reads=[('xt', tb % 2), 'cs'], writes=[('ps', pi)])
                    S.op('act', lambda e, pb=pb, cg=cg, tb=tb, rows=rows: e.activation(
                        out=x[:, cg * 4:cg * 4 + 4, tb * 128:tb * 128 + rows],
                        in_=pb[:, :].rearrange("p (c t) -> p c t", t=128)[:, :, 0:rows], func=AF.Copy),
                        reads=[('ps', pi)], writes=['x'])

            chk('c3')
            for l in range(DEPTH):
                j2 = l // 2
                phase()
                rmsnorm_to_h(l * 16, tiles)
                chk('c4')
                mix = h
                u = ar.take([128, 12, TT])
                dbg_u[0] = u
                if l % 2 == 0:
                    vtb = ar.take([128, 4, TW], BF16)
                    vts = ar.take([SS, TW]); vnb = ar.take([SS, TW], BF16)
                    ssq = ar.take([128, 5, 4])
                    off0 = ar.off
                    q = ar.take([128, 4, TT], BF16); pT = ar.take([128, 2, TP], BF16); rz = ar.take([128, TP])
                    vtmp = [ar.take([128, 512]) for _ in range(2)]; vsq = ar.take([128, 512])
                    if last:
                        xb_ = xattn_bufs()
                    nv = 0
                    for j in range(7):
                        wt, wkeys = wload(w_sgin[j2, j], 16 * 512)
                        w3 = wt[:, :].rearrange("p (k c) -> p k c", c=512)
                        if j < 3 or j == 6:
                            for mi in range(4):
                                for (t0, tn) in tiles:
                                    pi, pb = ps_next()
                                    mm(pb[:, 0:tn], [(w3[:, k, mi * 128:(mi + 1) * 128], h[:, k, t0:t0 + tn]) for k in range(16)], wkeys + ['h'], ('ps', pi))
                                    if j < 3:
                                        S.op('act', lambda e, pb=pb, m=4 * j + mi, t0=t0, tn=tn: e.activation(out=u[:, m, t0:t0 + tn], in_=pb[:, 0:tn], func=AF.Gelu_apprx_tanh),
                                             reads=[('ps', pi)], writes=['u'])
                                    else:
                                        S.op('act', lambda e, pb=pb, mi=mi, t0=t0, tn=tn: e.activation(out=q[:, mi, t0:t0 + tn], in_=pb[:, 0:tn], func=AF.Copy),
                                             reads=[('ps', pi)], writes=['q'])
                            if j == 6 and last:
                                xattn_sample(l, w3, wkeys, mix, xb_)
                        else:
                            for tb in range(nblk):
                                rows = 128 if tb < 4 else SS
                                pi, pb = ps_next()
                                mm(pb[0:rows, :], [(h[:, k, tb * 128:tb * 128 + rows], w3[:, k, :]) for k in range(16)], wkeys + ['h'], ('ps', pi))
                                vt_ = vtmp[nv % 2]; kv = ('vtmp', nv % 2); nv += 1
                                cols = slice((j - 3) * 512, (j - 2) * 512)
                                if tb < 4:
                                    S.op('act', lambda e, pb=pb, vt_=vt_: e.activation(out=vt_, in_=pb[:, :], func=AF.Gelu_apprx_tanh), reads=[('ps', pi)], writes=[kv])
                                    S.op('dve', lambda e, vt_=vt_, tb=tb, cols=cols: e.tensor_copy(out=vtb[:, tb, cols], in_=vt_), reads=[kv], writes=['vtb'])
                                    S.op('act', lambda e, vt_=vt_: e.activation(out=vsq, in_=vt_, func=AF.Square), reads=[kv], writes=['vsq'])
                                    S.op('dve', lambda e, tb=tb, j=j: e.tensor_reduce(out=ssq[:, tb, j - 3:j - 2], in_=vsq, axis=AX.X, op=ALU.add), reads=['vsq'], writes=['ssq'])
                                else:
                                    S.op('act', lambda e, pb=pb, cols=cols: e.activation(out=vts[:, cols], in_=pb[0:SS, :], func=AF.Gelu_apprx_tanh), reads=[('ps', pi)], writes=['vts'])
                                    S.op('act', lambda e, cols=cols: e.activation(out=vsq[0:SS, :], in_=vts[:, cols], func=AF.Square), reads=['vts'], writes=['vsq'])
                                    S.op('dve', lambda e, tb=tb, j=j: e.tensor_reduce(out=ssq[0:SS, tb, j - 3:j - 2], in_=vsq[0:SS, :], axis=AX.X, op=ALU.add), reads=['vsq'], writes=['ssq'])
                    chk('c5a')
                    xattn_prompt(l, q, mix, pT, rz)
                    chk('c5')
                    subphase(off0)
                    gvb = ar.take([128, TW]); bsb = ar.take([128, 12, 128]); wsb = ar.take([128, 12, 128], BF16)
                    w00t = ar.take([SS, 12]); dg = ar.take([SS, 12, SS], BF16); tmpg = ar.take([128, TP])
                    S.dma('sp', lambda e: e.dma_start(out=gvb, in_=gv_bc[j2]), writes=['gvb'])
                    S.dma('sp', lambda e: e.dma_start(out=bsb.rearrange("p a b -> p (a b)"), in_=bs_bc[j2]), writes=['bsb'])
                    S.dma('pool', lambda e: e.dma_start(out=wsb.rearrange("p a b -> p (a b)"), in_=wsT[j2]), writes=['wsb'])
                    for g in range(12):
                        S.op('dve', lambda e, g=g: e.tensor_tensor(out=wsb[:, g, :], in0=wsb[:, g, :], in1=tri, op=ALU.mult), reads=['wsb', 'cs'], writes=['wsb'])
                    if last:
                        S.dma('sp', lambda e: e.dma_start(out=w00t, in_=w00[j2]), writes=['w00t'])
                        for g in range(12):
                            S.op('dve', lambda e, g=g: e.tensor_scalar(out=dg[:, g, :], in0=ident[0:SS, 0:SS], scalar1=w00t[:, g:g + 1], scalar2=None, op0=ALU.mult),
                                 reads=['w00t', 'cs'], writes=['dg'])
                    for tb in range(nblk):
                        rows = 128 if tb < 4 else SS
                        S.op('dve', lambda e, tb=tb, rows=rows: e.tensor_reduce(out=ssq[0:rows, tb, 3:4], in_=ssq[0:rows, tb, 0:3], axis=AX.X, op=ALU.add), reads=['ssq'], writes=['ssq'])
                        S.op('act', lambda e, tb=tb, rows=rows: e.activation(out=ssq[0:rows, tb, 3:4], in_=ssq[0:rows, tb, 3:4], func=AF.Sqrt, bias=epsb[0:rows, 0:1], scale=1.0 / TW),
                             reads=['ssq', 'epsb'], writes=['ssq'])
                        S.op('dve', lambda e, tb=tb, rows=rows: e.reciprocal(out=ssq[0:rows, tb, 3:4], in_=ssq[0:rows, tb, 3:4]), reads=['ssq'], writes=['ssq'])
                        if tb < 4:
                            S.op('dve', lambda e, tb=tb: e.scalar_tensor_tensor(out=vtb[:, tb, :], in0=vtb[:, tb, :], scalar=ssq[:, tb, 3:4], in1=gvb, op0=ALU.mult, op1=ALU.mult),
                                 reads=['vtb', 'ssq', 'gvb'], writes=['vtb'])
                        else:
                            S.op('dve', lambda e, tb=tb: e.scalar_tensor_tensor(out=vts, in0=vts, scalar=ssq[0:SS, tb, 3:4], in1=gvb[0:SS, :], op0=ALU.mult, op1=ALU.mult),
                                 reads=['vts', 'ssq', 'gvb'], writes=['vts'])
                            S.op('dve', lambda e: e.tensor_copy(out=vnb, in_=vts), reads=['vts'], writes=['vnb'])
                            S.dma('sp', lambda e: e.dma_start(out=o_sgv[j2], in_=vts), reads=['vts'])
                    for g in range(12):
                        pi, pb = ps_next()
                        for tb in range(4):
                            S.op('pe', lambda e, g=g, tb=tb, pb=pb: e.matmul(pb[:, tb * 128:(tb + 1) * 128], lhsT=vtb[:, tb, g * 128:(g + 1) * 128], rhs=wsb[:, g, :], start=True, stop=True),
                                 reads=['vtb', 'wsb'], writes=[('ps', pi)])
                        for tb in range(4):
                            S.op('dve', lambda e, g=g, tb=tb, pb=pb: e.tensor_tensor(out=tmpg[:, tb * 128:(tb + 1) * 128], in0=pb[:, tb * 128:(tb + 1) * 128], in1=bsb[:, g, :], op=ALU.add),
                                 reads=[('ps', pi), 'bsb'], writes=['tmpg'])
                        S.op('dve', lambda e, g=g: e.tensor_tensor(out=mix[:, g, 0:TP], in0=tmpg, in1=u[:, g, 0:TP], op=ALU.mult), reads=['tmpg', 'u'], writes=['h'])
                        if last:
                            pi, pb = ps_next()
                            mm(pb[:, 0:SS], [(vnb[:, g * 128:(g + 1) * 128], dg[:, g, :])], ['vnb', 'dg'], ('ps', pi))
                            S.op('dve', lambda e, g=g, pb=pb: e.scalar_tensor_tensor(out=mix[:, g, TP:TT], in0=pb[:, 0:SS], scalar=bsb[:, g, 0:1], in1=u[:, g, TP:TT], op0=ALU.add, op1=ALU.mult),
                                 reads=[('ps', pi), 'bsb', 'u'], writes=['h'])
                    if STOP == 'c6v':
                        S.dma('pool', lambda e: e.dma_start(out=o_dbgh[:, 0:4 * TW], in_=vtb.rearrange("p a b -> p (a b)")), reads=['vtb'])
                        S.dma('sp', lambda e: e.dma_start(out=o_dbgu[:, 0:20], in_=ssq.rearrange("p a b -> p (a b)")), reads=['ssq'])
                        S.dma('sp', lambda e: e.dma_start(out=o_dbgx[0:SS, 0:TW], in_=vts), reads=['vts'])
                        S.dma('pool', lambda e: e.dma_start(out=o_dbgx[:, 2048:2048 + 12 * 128], in_=wsb.rearrange("p a b -> p (a b)")), reads=['wsb'])
                        S.stopped = True
                    chk('c6')
                    out_proj(w_sgout, j2, mix, tiles)
                    chk('c7')
                else:
                    ub = ar.take([128, 12, TT], BF16)
                    off0 = ar.off
                    q = ar.take([128, 4, TT], BF16); pT = ar.take([128, 2, TP], BF16); rz = ar.take([128, TP])
                    if last:
                        xb_ = xattn_bufs()
                    for j in range(4):
                        wt, wkeys = wload(w_ssin[j2, j], 16 * 512)
                        w3 = wt[:, :].rearrange("p (k c) -> p k c", c=512)
                        for mi in range(4):
                            for (t0, tn) in tiles:
                                pi, pb = ps_next()
                                mm(pb[:, 0:tn], [(w3[:, k, mi * 128:(mi + 1) * 128], h[:, k, t0:t0 + tn]) for k in range(16)], wkeys + ['h'], ('ps', pi))
                                if j < 3:
                                    m = 4 * j + mi
                                    S.op('act', lambda e, pb=pb, m=m, t0=t0, tn=tn: e.activation(out=u[:, m, t0:t0 + tn], in_=pb[:, 0:tn], func=AF.Copy), reads=[('ps', pi)], writes=[('u', m)])
                                    S.op('dve', lambda e, pb=pb, m=m, t0=t0, tn=tn: e.tensor_copy(out=ub[:, m, t0:t0 + tn], in_=pb[:, 0:tn]), reads=[('ps', pi)], writes=[('ub', m)])
                                else:
                                    S.op('act', lambda e, pb=pb, mi=mi, t0=t0, tn=tn: e.activation(out=q[:, mi, t0:t0 + tn], in_=pb[:, 0:tn], func=AF.Copy), reads=[('ps', pi)], writes=['q'])
                        if j == 3 and last:
                            xattn_sample(l, w3, wkeys, mix, xb_)
                    chk('c9')
                    xattn_prompt(l, q, mix, pT, rz)
                    subphase(off0)
                    bshape = [128, 8, 16]
                    Bc = ar.take(bshape); Bs = ar.take(bshape); Bt = ar.take(bshape); Bt2 = ar.take(bshape); tmpB = ar.take(bshape)
                    Cc = ar.take(bshape); Cs_ = ar.take(bshape)
                    Bpad = [ar.take([128, 8, 128], BF16) for _ in range(2)]
                    Cpad = [ar.take([128, 8 * 144], BF16) for _ in range(2)]
                    Tm = ar.take([128, 128])
                    angt = ar.take([128, TP]); kf = ar.take([128, TP]); ki = kf.bitcast(I32)
                    SgB = [ar.take([128, TP]) for _ in range(2)]; CgB = [ar.take([128, TP]) for _ in range(2)]
                    taB = [ar.take([128, TP]) for _ in range(2)]; tbB = [ar.take([128, TP]) for _ in range(2)]; WB = [ar.take([128, TP]) for _ in range(2)]
                    G1B = [ar.take([128, TP], BF16) for _ in range(2)]; G2B = [ar.take([128, TP], BF16) for _ in range(2)]
                    Wl = ar.take([128, NG]); JW = ar.take([128, NG]); tcar = ar.take([128, NG]); hfin = ar.take([128, NG])
                    if last:
                        s0T = ar.take([128, 8, SS]); s0sT = ar.take([128, 8, SS]); hs = ar.take([128, 8, SS]); hsb = ar.take([128, 8, SS], BF16)
                        stok1 = ar.take([SS, 8 * 128]); stok = [stok1, stok1]
                        hst = ar.take([SS, 4 * 128])
                    S.op('dve', lambda e: e.memset(Cpad[0], 0.0), writes=[('Cpad', 0)])
                    S.op('dve', lambda e: e.memset(Cpad[1], 0.0), writes=[('Cpad', 1)])
                    def stage1(g):
                        Sg = SgB[g % 2]; Cg = CgB[g % 2]; kS = ('Sg', g % 2); kC = ('Cg', g % 2)
                        S.op('act', lambda e: e.activation(out=angt, in_=iota, func=AF.Copy, scale=th[:, j2, g:g + 1]), reads=['cs', 'th'], writes=['angt'])
                        S.op('dve', lambda e: e.tensor_scalar(out=ki, in0=angt, scalar1=float(1 / TWO_PI), scalar2=None, op0=ALU.mult), reads=['angt'], writes=['kf'])
                        S.op('dve', lambda e: e.tensor_copy(out=kf, in_=ki), reads=['kf'], writes=['kf'])
                        S.op('dve', lambda e: e.scalar_tensor_tensor(out=angt, in0=kf, scalar=-TWO_PI, in1=angt, op0=ALU.mult, op1=ALU.add), reads=['kf', 'angt'], writes=['angt'])
                        S.op('act', lambda e: e.activation(out=Sg, in_=angt, func=AF.Sin, scale=-1.0), reads=['angt'], writes=[kS])
                        S.op('act', lambda e: e.activation(out=kf, in_=angt, func=AF.Abs), reads=['angt'], writes=['kf'])
                        S.op('act', lambda e: e.activation(out=Cg, in_=kf, func=AF.Sin, bias=hpib[:, 0:1], scale=-1.0), reads=['kf', 'hpib'], writes=[kC])

                    for j8 in range(12):
                        gs = slice(j8 * 128, (j8 + 1) * 128)
                        S.dma('sp', lambda e, gs=gs: e.dma_start(out=Bc.rearrange("p a b -> p (a b)"), in_=Bcat[j2][:, gs]), writes=['Bc'])
                        S.dma('sp', lambda e, gs=gs: e.dma_start(out=Bs.rearrange("p a b -> p (a b)"), in_=Bsw[j2][:, gs]), writes=['Bs'])
                        S.dma('sp', lambda e, gs=gs: e.dma_start(out=Cc.rearrange("p a b -> p (a b)"), in_=Ccat[j2][:, gs]), writes=['Cc'])
                        S.dma('sp', lambda e, gs=gs: e.dma_start(out=Cs_.rearrange("p a b -> p (a b)"), in_=Csw[j2][:, gs]), writes=['Cs'])
                        g8 = slice(j8 * 8, (j8 + 1) * 8)
                        S.op('dve', lambda e, g8=g8: e.tensor_tensor(out=Bt, in0=Bc, in1=KR[:, j2, g8].unsqueeze(2).to_broadcast(bshape), op=ALU.mult), reads=['Bc', 'KR'], writes=['Bt'])
                        S.op('dve', lambda e, g8=g8: e.tensor_tensor(out=tmpB, in0=Bs, in1=KIs[:, j2, g8].unsqueeze(2).to_broadcast(bshape), op=ALU.mult), reads=['Bs', 'KIs'], writes=['tmpB'])
                        S.op('dve', lambda e: e.tensor_tensor(out=Bt, in0=Bt, in1=tmpB, op=ALU.add), reads=['Bt', 'tmpB'], writes=['Bt'])
                        S.op('dve', lambda e, g8=g8: e.tensor_tensor(out=Bt2, in0=Bs, in1=KRs[:, j2, g8].unsqueeze(2).to_broadcast(bshape), op=ALU.mult), reads=['Bs', 'KRs'], writes=['Bt2'])
                        S.op('dve', lambda e, g8=g8: e.tensor_tensor(out=tmpB, in0=Bc, in1=KI[:, j2, g8].unsqueeze(2).to_broadcast(bshape), op=ALU.mult), reads=['Bc', 'KI'], writes=['tmpB'])
                        S.op('dve', lambda e: e.tensor_tensor(out=Bt2, in0=Bt2, in1=tmpB, op=ALU.subtract), reads=['Bt2', 'tmpB'], writes=['Bt2'])
                        for wi, Bsrc in ((0, Bt), (1, Bt2)):
                            pi, pb = ps_next()
                            mm(pb[:, 0:128], [(Bsrc.rearrange("p a b -> p (a b)"), ident)], ['Bt' if wi == 0 else 'Bt2', 'cs'], ('ps', pi))
                            S.op('act', lambda e, pb=pb: e.activation(out=Tm, in_=pb[:, 0:128], func=AF.Copy), reads=[('ps', pi)], writes=['Tm'])
                            for i in range(8):
                                S.op('dve', lambda e, wi=wi, i=i: e.tensor_scalar(out=Bpad[wi][:, i, :], in0=Tm, scalar1=cs[:, C_GMASK + i:C_GMASK + i + 1], scalar2=None, op0=ALU.mult),
                                     reads=['Tm', 'cs'], writes=[('Bpad', wi)])
                        for wi, Csrc in ((0, Cc), (1, Cs_)):
                            S.op('dve', lambda e, wi=wi, Csrc=Csrc: e.tensor_copy(out=Cpad[wi].rearrange("p (i s) -> p i s", s=144)[:, :, 0:16], in_=Csrc),
                                 reads=['Cc' if wi == 0 else 'Cs'], writes=[('Cpad', wi)])
                        if last:
                            for nm, src, dstT in (('cat', ss_cat, s0T), ('sw', ss_sw, s0sT)):
                                sk = stok[0 if nm == 'cat' else 1]
                                S.dma('sp', lambda e, src=src, sk=sk: e.dma_start(out=sk, in_=src[j2][:, j8 * 1024:(j8 + 1) * 1024]), writes=['stok'])
                                pi, pb = ps_next()
                                for gi in range(8):
                                    S.op('pe', lambda e, gi=gi, pb=pb, sk=sk: e.matmul(pb[:, gi * SS:(gi + 1) * SS], lhsT=sk[:, gi * 128:(gi + 1) * 128], rhs=ident[0:SS, 0:SS], start=True, stop=True),
                                         reads=['stok', 'cs'], writes=[('ps', pi)])
                                S.op('act', lambda e, pb=pb, dstT=dstT: e.activation(out=dstT, in_=pb[:, 0:8 * SS].rearrange("p (g b) -> p g b", b=SS), func=AF.Copy),
                                     reads=[('ps', pi)], writes=['s0' + nm])
                        pby = pbank[6]; pbys = pbank[7]
                        pending = [None]
                        for i in range(8):
                            g = j8 * 8 + i
                            if i == 0 and j8 == 0:
                                stage1(0)
                            if g + 1 < NG:
                                stage1(g + 1)
                            Sg = SgB[g % 2]; Cg = CgB[g % 2]; kS = ('Sg', g % 2); kC = ('Cg', g % 2)
                            ta = taB[g % 2]; tb_ = tbB[g % 2]; W = WB[g % 2]; G1 = G1B[g % 2]; G2 = G2B[g % 2]
                            kta = ('ta', g % 2); ktb = ('tb', g % 2); kW = ('W', g % 2); kG1 = ('G1', g % 2); kG2 = ('G2', g % 2)
                            pi0, pb0 = ps_next()
                            mm(pb0[:, :], [(Bpad[0][:, i, :], ub[:, j8, 0:TP])], [('Bpad', 0), ('ub', j8)], ('ps', pi0))
                            pi1, pb1 = ps_next()
                            mm(pb1[:, :], [(Bpad[1][:, i, :], ub[:, j8, 0:TP])], [('Bpad', 1), ('ub', j8)], ('ps', pi1))
                            S.op('dve', lambda e, pb0=pb0: e.tensor_tensor(out=ta, in0=pb0[:, :], in1=Cg, op=ALU.mult), reads=[('ps', pi0), kC], writes=[kta])
                            S.op('dve', lambda e, pb1=pb1: e.tensor_tensor(out=tb_, in0=pb1[:, :], in1=Sg, op=ALU.mult), reads=[('ps', pi1), kS], writes=[ktb])
                            S.op('dve', lambda e: e.tensor_tensor(out=ta, in0=ta, in1=tb_, op=ALU.add), reads=[kta, ktb], writes=[kta])
                            S.op('dve', lambda e, g=g: e.tensor_tensor_scan(out=W, data0=rho[:, j2, g:g + 1].to_broadcast([128, TP]), data1=ta, initial=winit[:, j2, g:g + 1], op0=ALU.mult, op1=ALU.add),
                                 reads=[kta, 'rho', 'winit'], writes=[kW])
                            S.op('dve', lambda e: e.scalar_tensor_tensor(out=G1, in0=W, scalar=cs[:, C_SGNC:C_SGNC + 1], in1=Cg, op0=ALU.mult, op1=ALU.mult), reads=[kW, kC, 'cs'], writes=[kG1])
                            S.op('pool', lambda e: e.tensor_tensor(out=G2, in0=W, in1=Sg, op=ALU.mult), reads=[kW, kS], writes=[kG2])
                            S.op('act', lambda e, g=g: e.activation(out=Wl[:, g:g + 1], in_=W[:, TP - 1:TP], func=AF.Copy), reads=[kW], writes=['Wl'])
                            fns = [lambda e, i=i: e.matmul(pby[:, :], lhsT=Cpad[0][:, i * 128:(i + 1) * 128], rhs=G1, start=(i == 0), stop=False),
                                   lambda e, i=i: e.matmul(pby[:, :], lhsT=Cpad[1][:, i * 128:(i + 1) * 128], rhs=G2, start=False, stop=(i == 7))]
                            S.op('pe', fns, reads=[('Cpad', 0), ('Cpad', 1), kG1, kG2], writes=[('ps', 6)])
                            if last:
                                pis, pbs = ps_next()
                                mm(pbs[:, 0:SS], [(Bpad[0][:, i, :], ub[:, j8, TP:TT])], [('Bpad', 0), ('ub', j8)], ('ps', pis))

                            def sample_ops(g=g, i=i, pis=(pis if last else None), pbs=(pbs if last else None)):
                                    S.op('dve', lambda e, g=g, i=i, pbs=pbs: e.scalar_tensor_tensor(out=hs[:, i, :], in0=s0T[:, i, :], scalar=A1[:, j2, g:g + 1], in1=pbs[:, 0:SS], op0=ALU.mult, op1=ALU.add),
                                         reads=['s0cat', 'A1', ('ps', pis)], writes=['hs'])
                                    S.op('dve', lambda e, g=g, i=i: e.scalar_tensor_tensor(out=hs[:, i, :], in0=s0sT[:, i, :], scalar=A2s[:, j2, g:g + 1], in1=hs[:, i, :], op0=ALU.mult, op1=ALU.add),
                                         reads=['s0sw', 'A2s', 'hs'], writes=['hs'])
                                    S.op('dve', lambda e, i=i: e.tensor_scalar(out=hsb[:, i, :], in0=hs[:, i, :], scalar1=cs[:, C_SGNC:C_SGNC + 1], scalar2=None, op0=ALU.mult),
                                         reads=['hs', 'cs'], writes=['hsb'])
                                    S.op('pe', lambda e, i=i: e.matmul(pbys[:, 0:SS], lhsT=Cpad[0][:, i * 128:(i + 1) * 128], rhs=hsb[:, i, :], start=(i == 0), stop=(i == 7)),
                                         reads=[('Cpad', 0), 'hsb'], writes=[('ps', 7)])
                            if last:
                                if pending[0] is not None:
                                    pending[0]()
                                pending[0] = sample_ops
                        if last:
                            pending[0](); pending[0] = None
                            for half in range(2):
                                pi, pb = ps_next()
                                for gi in range(4):
                                    S.op('pe', lambda e, gi=gi, half=half, pb=pb: e.matmul(pb[0:SS, gi * 128:(gi + 1) * 128], lhsT=hs[:, half * 4 + gi, :], rhs=ident, start=True, stop=True),
                                         reads=['hs', 'cs'], writes=[('ps', pi)])
                                S.op('act', lambda e, pb=pb, half=half: e.activation(out=hst, in_=pb[0:SS, :], func=AF.Copy), reads=[('ps', pi)], writes=['hst'])
                                S.dma('sp', lambda e, j8=j8, half=half: e.dma_start(out=o_ssm_s[j2][:, j8 * 1024 + half * 512:j8 * 1024 + (half + 1) * 512], in_=hst), reads=['hst'])
                        ytl = [(0, TP, pby, 6)] + ([(TP, SS, pbys, 7)] if last else [])
                        for (t0, tn, pbb, pii) in ytl:
                            S.op('dve', lambda e, t0=t0, tn=tn, pbb=pbb: e.scalar_tensor_tensor(out=u[:, j8, t0:t0 + tn], in0=u[:, j8, t0:t0 + tn], scalar=dv[:, j2, j8:j8 + 1], in1=pbb[:, 0:tn], op0=ALU.mult, op1=ALU.add),
                                 reads=[('u', j8), 'dv', ('ps', pii)], writes=[('u', j8)])
                            S.op('act', lambda e, t0=t0, tn=tn: e.activation(out=u[:, j8, t0:t0 + tn], in_=u[:, j8, t0:t0 + tn], func=AF.Gelu_apprx_tanh), reads=[('u', j8)], writes=[('u', j8)])
                            S.op('dve', lambda e, t0=t0, tn=tn: e.tensor_copy(out=ub[:, j8, t0:t0 + tn], in_=u[:, j8, t0:t0 + tn]), reads=[('u', j8)], writes=[('ub', j8)])
                    chk('c10')
                    pi, pb = ps_next()
                    mm(pb[:, 0:NG], [(JT, Wl)], ['cs', 'Wl'], ('ps', pi))
                    S.op('act', lambda e, pb=pb: e.activation(out=JW, in_=pb[:, 0:NG], func=AF.Copy), reads=[('ps', pi)], writes=['JW'])
                    if last:
                        S.op('dve', lambda e: e.tensor_tensor(out=hfin, in0=Wl, in1=c511[:, j2, :], op=ALU.mult), reads=['Wl', 'c511'], writes=['hfin'])
                        S.op('dve', lambda e: e.tensor_tensor(out=tcar, in0=JW, in1=s511[:, j2, :], op=ALU.mult), reads=['JW', 's511'], writes=['tcar'])
                        S.op('dve', lambda e: e.tensor_tensor(out=hfin, in0=hfin, in1=tcar, op=ALU.add), reads=['hfin', 'tcar'], writes=['hfin'])
                        S.dma('sp', lambda e: e.dma_start(out=o_ssm_p[j2], in_=hfin), reads=['hfin'])
                    else:
                        S.op('dve', lambda e: e.tensor_tensor(out=tcar, in0=JW, in1=s512[:, j2, :], op=ALU.mult), reads=['JW', 's512'], writes=['tcar'])
                        S.op('dve', lambda e: e.tensor_tensor(out=hfin, in0=Wl, in1=c512[:, j2, :], op=ALU.mult), reads=['Wl', 'c512'], writes=['hfin'])
                        S.op('dve', lambda e: e.tensor_tensor(out=winit[:, j2, :], in0=hfin, in1=tcar, op=ALU.add), reads=['hfin', 'tcar', ('W', 0), ('W', 1)], writes=['winit'])
                    for j in range(3):
                        wt, wkeys = wload(w_glu[j2, j], 12 * 512)
                        w3 = wt[:, 0:12 * 512].rearrange("p (k c) -> p k c", c=512)
                        for mi in range(4):
                            m = 4 * j + mi
                            for (t0, tn) in tiles:
                                pi, pb = ps_next()
                                mm(pb[:, 0:tn], [(w3[:, k, mi * 128:(mi + 1) * 128], ub[:, k, t0:t0 + tn]) for k in range(12)], wkeys + [('ub', k) for k in range(12)], ('ps', pi))
                                S.op('act', lambda e, pb=pb, m=m, t0=t0, tn=tn: e.activation(out=rsd[:, t0:t0 + tn], in_=pb[:, 0:tn], func=AF.Sigmoid, bias=bg[:, j2, m:m + 1], scale=1.0),
                                     reads=[('ps', pi), 'bg'], writes=['rsd'])
                                S.op('dve', lambda e, m=m, t0=t0, tn=tn: e.tensor_tensor(out=mix[:, m, t0:t0 + tn], in0=u[:, m, t0:t0 + tn], in1=rsd[:, t0:t0 + tn], op=ALU.mult),
                                     reads=[('u', m), 'rsd'], writes=['h'])
                    chk('c11')
                    out_proj(w_ssout, j2, mix, tiles)

                chk('f%d' % l)
                phase()
                rmsnorm_to_h(64 + l * 16, tiles)
                y = ar.take([128, FC, TT], BF16)
                abuf = [ar.take([128, 2 + TT]) for _ in range(2)]
                cbuf = [ar.take([128, TT]) for _ in range(2)]
                if last:
                    pvT = ar.take([128, FC, 2, SS])
                    sct = [ar.take([SS, 2, 128]) for _ in range(2)]
                    cxt = [ar.take([18, 128]) for _ in range(2)]
                    S.dma('sp', lambda e: e.dma_start(out=o_conv_s0[l], in_=sconv[l][:, 1, :]))
                    for c in range(FC):
                        sk = sct[c % 2]
                        S.dma('sp', lambda e, c=c, sk=sk: e.dma_start(out=sk, in_=sconv[l][:, :, c * 128:(c + 1) * 128]), writes=[('sct', c % 2)])
                        pi, pb = ps_next()
                        for r_ in range(2):
                            S.op('pe', lambda e, r_=r_, pb=pb, sk=sk: e.matmul(pb[:, r_ * SS:(r_ + 1) * SS], lhsT=sk[:, r_, :], rhs=ident[0:SS, 0:SS], start=True, stop=True),
                                 reads=[('sct', c % 2), 'cs'], writes=[('ps', pi)])
                        S.op('act', lambda e, c=c, pb=pb: e.activation(out=pvT[:, c, :, :], in_=pb[:, 0:2 * SS].rearrange("p (r b) -> p r b", b=SS), func=AF.Copy), reads=[('ps', pi)], writes=['pvT'])
                for j in range(22):
                    wt, wkeys = wload(w_up[l, j], 16 * 512)
                    w3 = wt[:, :].rearrange("p (k c) -> p k c", c=512)
                    for ci in range(2 if j < 21 else 1):
                        c = 2 * j + ci
                        ab = abuf[c % 2]; cb = cbuf[c % 2]
                        cwc = cw[:, l, c * 4:(c + 1) * 4]
                        pia, pba = ps_next()
                        mm(pba[:, :], [(w3[:, k, ci * 128:(ci + 1) * 128], h[:, k, 0:TP]) for k in range(16)], wkeys + ['h'], ('ps', pia))
                        pig, pbg = ps_next()
                        mm(pbg[:, :], [(w3[:, k, 256 + ci * 128:256 + (ci + 1) * 128], h[:, k, 0:TP]) for k in range(16)], wkeys + ['h'], ('ps', pig))
                        S.op('act', lambda e, ab=ab, c=c: e.activation(out=ab[:, 0:2], in_=halo[:, l, c, :], func=AF.Copy), reads=['halo'], writes=[('ab', c % 2)])
                        S.op('act', lambda e, ab=ab, pba=pba: e.activation(out=ab[:, 2:2 + TP], in_=pba[:, :], func=AF.Copy), reads=[('ps', pia)], writes=[('ab', c % 2)])
                        S.op('act', lambda e, ab=ab, c=c: e.activation(out=halo[:, l, c, :], in_=ab[:, TP:TP + 2], func=AF.Copy), reads=[('ab', c % 2)], writes=['halo'])
                        S.op('dve', lambda e, ab=ab, cb=cb, cwc=cwc: e.tensor_scalar(out=cb[:, 0:TP], in0=ab[:, 2:2 + TP], scalar1=cwc[:, 2:3], scalar2=cwc[:, 3:4], op0=ALU.mult, op1=ALU.add),
                             reads=[('ab', c % 2), 'cw'], writes=[('cb', c % 2)])
                        S.op('dve', lambda e, ab=ab, cb=cb, cwc=cwc: e.scalar_tensor_tensor(out=cb[:, 0:TP], in0=ab[:, 1:1 + TP], scalar=cwc[:, 1:2], in1=cb[:, 0:TP], op0=ALU.mult, op1=ALU.add),
                             reads=[('ab', c % 2), 'cw', ('cb', c % 2)], writes=[('cb', c % 2)])
                        S.op('dve', lambda e, ab=ab, cb=cb, cwc=cwc: e.scalar_tensor_tensor(out=cb[:, 0:TP], in0=ab[:, 0:TP], scalar=cwc[:, 0:1], in1=cb[:, 0:TP], op0=ALU.mult, op1=ALU.add),
                             reads=[('ab', c % 2), 'cw', ('cb', c % 2)], writes=[('cb', c % 2)])
                        S.op('act', lambda e, cb=cb: e.activation(out=cb[:, 0:TP], in_=cb[:, 0:TP], func=AF.Silu), reads=[('cb', c % 2)], writes=[('cb', c % 2)])
                        S.op('dve', lambda e, cb=cb, pbg=pbg, c=c: e.tensor_tensor(out=y[:, c, 0:TP], in0=cb[:, 0:TP], in1=pbg[:, :], op=ALU.mult), reads=[('cb', c % 2), ('ps', pig)], writes=[('y', c)])
                        if last:
                            pia, pba = ps_next()
                            mm(pba[:, 0:SS], [(w3[:, k, ci * 128:(ci + 1) * 128], h[:, k, TP:TT]) for k in range(16)], wkeys + ['h'], ('ps', pia))
                            pig, pbg = ps_next()
                            mm(pbg[:, 0:SS], [(w3[:, k, 256 + ci * 128:256 + (ci + 1) * 128], h[:, k, TP:TT]) for k in range(16)], wkeys + ['h'], ('ps', pig))
                            S.op('act', lambda e, ab=ab, pba=pba: e.activation(out=ab[:, 2 + TP:2 + TT], in_=pba[:, 0:SS], func=AF.Copy), reads=[('ps', pia)], writes=[('ab', c % 2)])
                            S.op('dve', lambda e, ab=ab, cb=cb, cwc=cwc: e.tensor_scalar(out=cb[:, TP:TT], in0=ab[:, 2 + TP:2 + TT], scalar1=cwc[:, 2:3], scalar2=cwc[:, 3:4], op0=ALU.mult, op1=ALU.add),
                                 reads=[('ab', c % 2), 'cw'], writes=[('cb', c % 2)])
                            S.op('dve', lambda e, cb=cb, cwc=cwc, c=c: e.scalar_tensor_tensor(out=cb[:, TP:TT], in0=pvT[:, c, 1, :], scalar=cwc[:, 1:2], in1=cb[:, TP:TT], op0=ALU.mult, op1=ALU.add),
                                 reads=['pvT', 'cw', ('cb', c % 2)], writes=[('cb', c % 2)])
                            S.op('dve', lambda e, cb=cb, cwc=cwc, c=c: e.scalar_tensor_tensor(out=cb[:, TP:TT], in0=pvT[:, c, 0, :], scalar=cwc[:, 0:1], in1=cb[:, TP:TT], op0=ALU.mult, op1=ALU.add),
                                 reads=['pvT', 'cw', ('cb', c % 2)], writes=[('cb', c % 2)])
                            S.op('act', lambda e, cb=cb: e.activation(out=cb[:, TP:TT], in_=cb[:, TP:TT], func=AF.Silu), reads=[('cb', c % 2)], writes=[('cb', c % 2)])
                            S.op('dve', lambda e, cb=cb, pbg=pbg, c=c: e.tensor_tensor(out=y[:, c, TP:TT], in0=cb[:, TP:TT], in1=pbg[:, 0:SS], op=ALU.mult), reads=[('cb', c % 2), ('ps', pig)], writes=[('y', c)])
                            pit, pbt = ps_next()
                            mm(pbt[0:18, 0:128], [(ab[:, TP:TP + 18], ident)], [('ab', c % 2), 'cs'], ('ps', pit))
                            cx = cxt[c % 2]
                            S.op('act', lambda e, pbt=pbt, cx=cx: e.activation(out=cx, in_=pbt[0:18, 0:128], func=AF.Copy), reads=[('ps', pit)], writes=[('cxt', c % 2)])
                            S.dma('sp', lambda e, cx=cx, c=c: e.dma_start(out=o_convx[l][:, c * 128:(c + 1) * 128], in_=cx), reads=[('cxt', c % 2)])
                ykeys = [('y', c) for c in range(FC)]
                for m in range(16):
                    wt, wkeys = wload(w_dn[l, m], FC * 128)
                    w3 = wt[:, 0:FC * 128].rearrange("p (k c) -> p k c", c=128)
                    for (t0, tn) in tiles:
                        pi, pb = ps_next()
                        mm(pb[:, 0:tn], [(w3[:, k, :], y[:, k, t0:t0 + tn]) for k in range(FC)], wkeys + ykeys, ('ps', pi))
                        S.op('dve', lambda e, pb=pb, m=m, t0=t0, tn=tn: e.tensor_tensor(out=x[:, m, t0:t0 + tn], in0=x[:, m, t0:t0 + tn], in1=pb[:, 0:tn], op=ALU.add),
                             reads=[('ps', pi), 'x'], writes=['x'])

            chk('ffn3')
            phase()
            yo = ar.take([128, 16, TT])
            rmsnorm_to_h(192, tiles, out_fp32=yo)
            ot = [ar.take([128, D]) for _ in range(2)]
            for tb in range(nblk):
                rows = 128 if tb < 4 else SS
                buf = ot[tb % 2]
                for cg in range(4):
                    pi, pb = ps_next()
                    for ci in range(4):
                        c = cg * 4 + ci
                        S.op('pe', lambda e, c=c, ci=ci, tb=tb, rows=rows, pb=pb: e.matmul(pb[0:rows, ci * 128:(ci + 1) * 128], lhsT=yo[:, c, tb * 128:tb * 128 + rows], rhs=ident, start=True, stop=True),
                             reads=['yout', 'cs'], writes=[('ps', pi)])
                    S.op('act', lambda e, pb=pb, buf=buf, cg=cg, rows=rows: e.activation(out=buf[0:rows, cg * 512:(cg + 1) * 512], in_=pb[0:rows, :], func=AF.Copy), reads=[('ps', pi)], writes=[('ot', tb % 2)])
                dst = o_y[p * TP + tb * 128:p * TP + tb * 128 + 128, :] if tb < 4 else o_ys
                S.dma('sp', lambda e, buf=buf, dst=dst, rows=rows: e.dma_start(out=dst, in_=buf[0:rows, :]), reads=[('ot', tb % 2)])

        print('arena peak', ar.peak, 'of', ARENA)
        print('instructions recorded:', S.nins, {e: len(S.prog[e]) for e in S.prog})
        S.emit(block)
    return nc


def _blk(W, bw):
    K, N = W.shape
    kc = K // 128
    nb = N // bw
    a = W.reshape(kc, 128, nb, bw).transpose(2, 1, 0, 3)
    return np.ascontiguousarray(a).reshape(nb, 128, kc * bw)


def _fm(v):
    return np.ascontiguousarray(v.reshape(-1, 128).T)


_CACHE = {}


def prep_small(g_mix, g_ffn, g_mem, g_final, sg_g_v, sg_w_s, sg_b_s, ssm_lam_re, ssm_lam_im, ssm_log_dt, ssm_b_re, ssm_b_im,
               ssm_c_re, ssm_c_im, ssm_d, ssm_b_glu, ffn_conv_w, ffn_conv_b):
    f = np.float32
    A = lambda a: np.asarray(a, dtype=f)
    sh = {}
    gvec = np.zeros((128, 13 * 16), f)
    for l in range(4):
        gvec[:, l * 16:(l + 1) * 16] = _fm(A(g_mix)[l])
        gvec[:, 64 + l * 16:64 + (l + 1) * 16] = _fm(A(g_ffn)[l])
        gvec[:, 128 + l * 16:128 + (l + 1) * 16] = _fm(A(g_mem)[l])
    gvec[:, 192:208] = _fm(A(g_final))
    sh['gvec'] = gvec
    sh['gv_bc'] = np.ascontiguousarray(np.broadcast_to(A(sg_g_v)[:, None, :], (2, 128, TW)))
    sh['wsT'] = np.ascontiguousarray(A(sg_w_s).transpose(0, 3, 1, 2)).reshape(2, 128, 12 * 128)
    sh['bs_bc'] = np.ascontiguousarray(np.broadcast_to(A(sg_b_s).reshape(2, 1, 12 * 128), (2, 128, 12 * 128)))
    sh['w00'] = np.ascontiguousarray(np.broadcast_to(A(sg_w_s)[:, None, :, 0, 0], (2, 16, 12)))
    lr = A(ssm_lam_re).transpose(0, 2, 1); li = A(ssm_lam_im).transpose(0, 2, 1)
    lamx = np.zeros((2, 128, 3, NG), f)
    lamx[:, :, 0, :] = np.concatenate([lr, lr], axis=1)
    lamx[:, :, 1, :] = np.concatenate([li, li], axis=1)
    lamx[:, :, 2, :] = A(ssm_log_dt)[:, None, :]
    sh['lam'] = lamx.reshape(2, 128, 3 * NG)
    bre = A(ssm_b_re).transpose(0, 2, 1, 3); bim = A(ssm_b_im).transpose(0, 2, 1, 3)
    sh['Bcat'] = np.ascontiguousarray(np.concatenate([bre, bim], axis=1)).reshape(2, 128, NG * 16)
    sh['Bsw'] = np.ascontiguousarray(np.concatenate([bim, bre], axis=1)).reshape(2, 128, NG * 16)
    cre = A(ssm_c_re).transpose(0, 3, 1, 2); cim = A(ssm_c_im).transpose(0, 3, 1, 2)
    sh['Ccat'] = np.ascontiguousarray(np.concatenate([cre, cim], axis=1)).reshape(2, 128, NG * 16)
    sh['Csw'] = np.ascontiguousarray(np.concatenate([cim, cre], axis=1)).reshape(2, 128, NG * 16)
    sh['dvec'] = np.stack([_fm(A(ssm_d)[l]) for l in range(2)])
    sh['bglu'] = np.stack([_fm(A(ssm_b_glu)[l]) for l in range(2)])
    cwv = np.zeros((4, 128, FC, 4), f)
    for l in range(4):
        for j in range(3):
            cwv[l, :, :, j] = _fm(A(ffn_conv_w)[l, j])
        cwv[l, :, :, 3] = _fm(A(ffn_conv_b)[l])
    sh['convw'] = cwv.reshape(4, 128, FC * 4)
    sh['cst'] = make_consts()
    return sh


def prep_core(c, b, x_prompt, x_sample, mem_prompt, cache_mem_k, cache_mem_v, state_ssm_re, state_ssm_im, state_conv):
    m = {}
    m['xp'] = np.ascontiguousarray(x_prompt[b])
    m['xs'] = np.ascontiguousarray(x_sample[c * SS:(c + 1) * SS, 0, :])
    m['mem'] = np.ascontiguousarray(mem_prompt[b])
    m['ck'] = np.ascontiguousarray(cache_mem_k[:, c * SS:(c + 1) * SS].reshape(4, SS, NMEM, 512))
    m['cv'] = np.ascontiguousarray(cache_mem_v[:, c * SS:(c + 1) * SS].reshape(4, SS, NMEM, 512))
    sre = state_ssm_re[:, c * SS:(c + 1) * SS]; sim = state_ssm_im[:, c * SS:(c + 1) * SS]
    m['ss_cat'] = np.ascontiguousarray(np.concatenate([sre, sim], axis=-1)).reshape(2, SS, NG * 128)
    m['ss_sw'] = np.ascontiguousarray(np.concatenate([sim, sre], axis=-1)).reshape(2, SS, NG * 128)
    m['sconv'] = np.ascontiguousarray(state_conv[:, c * SS:(c + 1) * SS])
    return m


def kernel(x_prompt, x_sample, mem_prompt, cache_mem_k, cache_mem_v, state_ssm_re, state_ssm_im,
           state_conv, g_mix, g_ffn, g_mem, g_final, w_mem_kv, sg_w_in, sg_w_out, sg_g_v, sg_w_s,
           sg_b_s, ssm_w_in, ssm_w_out, ssm_lam_re, ssm_lam_im, ssm_log_dt, ssm_b_re, ssm_b_im,
           ssm_c_re, ssm_c_im, ssm_d, ssm_w_glu, ssm_b_glu, ffn_w_up, ffn_conv_w, ffn_conv_b,
           ffn_w_down):
    f = np.float32
    A = lambda a: np.asarray(a, dtype=f)
    x_prompt, x_sample, mem_prompt = A(x_prompt), A(x_sample), A(mem_prompt)
    cache_mem_k, cache_mem_v = A(cache_mem_k), A(cache_mem_v)
    state_ssm_re, state_ssm_im, state_conv = A(state_ssm_re), A(state_ssm_im), A(state_conv)
    sh = prep_small(g_mix, g_ffn, g_mem, g_final, sg_g_v, sg_w_s, sg_b_s, ssm_lam_re, ssm_lam_im, ssm_log_dt, ssm_b_re, ssm_b_im,
                    ssm_c_re, ssm_c_im, ssm_d, ssm_b_glu, ffn_conv_w, ffn_conv_b)
    sh['w_kv'] = np.stack([_blk(A(w_mem_kv)[l], 512) for l in range(4)])
    sh['w_sgin'] = np.stack([_blk(A(sg_w_in)[l], 512) for l in range(2)])
    sh['w_sgout'] = np.stack([_blk(A(sg_w_out)[l], 512) for l in range(2)])
    sh['w_ssin'] = np.stack([_blk(A(ssm_w_in)[l], 512) for l in range(2)])
    sh['w_ssout'] = np.stack([_blk(A(ssm_w_out)[l], 512) for l in range(2)])
    sh['w_glu'] = np.stack([_blk(A(ssm_w_glu)[l], 512) for l in range(2)])
    wup = np.zeros((4, 22, 128, 16, 512), f)
    for l in range(4):
        W = A(ffn_w_up)[l]
        Wa = np.zeros((D, 22 * 256), f); Wg = np.zeros((D, 22 * 256), f)
        Wa[:, :DFF] = W[:, :DFF]; Wg[:, :DFF] = W[:, DFF:]
        wup[l, :, :, :, 0:256] = _blk(Wa, 256).reshape(22, 128, 16, 256)
        wup[l, :, :, :, 256:512] = _blk(Wg, 256).reshape(22, 128, 16, 256)
    sh['w_up'] = wup.reshape(4, 22, 128, 16 * 512)
    sh['w_dn'] = np.stack([_blk(A(ffn_w_down)[l], 128) for l in range(4)])
    in_maps = []
    for c in range(8):
        m = dict(sh)
        m.update(prep_core(c, c % NB, x_prompt, x_sample, mem_prompt, cache_mem_k, cache_mem_v, state_ssm_re, state_ssm_im, state_conv))
        in_maps.append(m)
    if 'nc' not in _CACHE:
        _CACHE['nc'] = build_program()
    nc = _CACHE['nc']
    res = run_bass_kernel_spmd(nc, in_maps, core_ids=list(range(8)))
    R = res.results
    y_prompt = np.stack([R[b]['o_y'] for b in range(NB)]).astype(f)
    y_sample = np.concatenate([R[c]['o_ys'] for c in range(8)], axis=0).reshape(NS, 1, D).astype(f)
    mk = np.stack([R[b]['o_mk'] for b in range(NB)], axis=1).reshape(4, NB, NMEM, 4, 128).astype(f)
    mv = np.stack([R[b]['o_mv'] for b in range(NB)], axis=1).reshape(4, NB, NMEM, 4, 128).astype(f)
    sp = np.stack([R[b]['o_ssm_p'] for b in range(NB)], axis=1)
    ssm_re_p = np.ascontiguousarray(sp[:, :, 0:64, :].transpose(0, 1, 3, 2)).astype(f)
    ssm_im_p = np.ascontiguousarray(sp[:, :, 64:128, :].transpose(0, 1, 3, 2)).astype(f)
    conv_p = np.stack([R[b]['o_convx'][:, 0:2, :] for b in range(NB)], axis=1).astype(f)
    ss_ = np.concatenate([R[c]['o_ssm_s'] for c in range(8)], axis=1).reshape(2, NS, NG, 128)
    ssm_re_s = np.ascontiguousarray(ss_[..., 0:64]).astype(f)
    ssm_im_s = np.ascontiguousarray(ss_[..., 64:128]).astype(f)
    c0 = np.concatenate([R[c]['o_conv_s0'] for c in range(8)], axis=1)
    c1 = np.concatenate([R[c]['o_convx'][:, 2:18, :] for c in range(8)], axis=1)
    conv_s = np.stack([c0, c1], axis=2).astype(f)
    sgv = np.concatenate([R[c]['o_sgv'] for c in range(8)], axis=1).reshape(2, NS, 1, TW).astype(f)
    return (y_prompt, y_sample, mk, mv, ssm_re_p, ssm_im_p, conv_p, ssm_re_s, ssm_im_s, conv_s, sgv)
```
